# Optimizing a Trainium2 kernel written in Bass

```python
import math
import jax, jax.numpy as jnp
from jax import lax
import numpy as np

D_MODEL = 1024
BATCH = 4
SEQ = 4096
DEPTH = 1

N_MEM = 256
D_MIX = D_MODEL
DN_HEADS = 4
DN_HEAD_DIM = 128
DN_WIDTH = DN_HEADS * DN_HEAD_DIM
CONV_WIDTH = 4
CHUNK = 64
S5_WIDTH = D_MIX - DN_WIDTH
S5_CH_PER_GROUP = 16
S5_GROUPS = S5_WIDTH // S5_CH_PER_GROUP
S5_STATE = 64
X_HEADS = 4
X_HEAD_DIM = D_MODEL // X_HEADS
D_FF = -(-8 * D_MODEL // (3 * 256)) * 256
EPS = 1e-6

OFF_Q = 0
OFF_K = OFF_Q + DN_WIDTH
OFF_V = OFF_K + DN_WIDTH
OFF_Z = OFF_V + DN_WIDTH
OFF_A = OFF_Z + DN_WIDTH
OFF_B = OFF_A + DN_HEADS
OFF_U = OFF_B + DN_HEADS
D_IN = OFF_U + S5_WIDTH

kernel_name = "hymba_deltanet_s5_memxattn_block"


def _rmsnorm(x, gain):
    x32 = x.astype(jnp.float32)
    y = x32 * lax.rsqrt(jnp.mean(x32 * x32, axis=-1, keepdims=True) + EPS)
    return (y * gain.astype(jnp.float32)).astype(x.dtype)


def _l2norm(x):
    return x * lax.rsqrt(jnp.sum(x * x, axis=-1, keepdims=True) + EPS)


def _causal_dwconv(x, w):
    c = x.shape[-1]
    return lax.conv_general_dilated(
        x, w.astype(x.dtype)[:, None, :], window_strides=(1,),
        padding=((CONV_WIDTH - 1, 0),),
        dimension_numbers=("NWC", "WIO", "NWC"), feature_group_count=c)


def _gated_delta_rule(q, k, v, g, beta):
    bsz, t, h, dk = q.shape
    dv = v.shape[-1]
    n = t // CHUNK

    def chunk(a):
        a = a.reshape((bsz, n, CHUNK, h) + a.shape[3:])
        return jnp.moveaxis(a, 3, 2)

    q, k, v, g, beta = chunk(q), chunk(k), chunk(v), chunk(g), chunk(beta)
    g_cum = jnp.cumsum(g, axis=-1)
    causal = jnp.tril(jnp.ones((CHUNK, CHUNK), dtype=bool))
    strict = jnp.tril(jnp.ones((CHUNK, CHUNK), dtype=bool), k=-1)
    decay = jnp.exp(jnp.where(causal, g_cum[..., :, None] - g_cum[..., None, :], -jnp.inf))

    kb = k * beta[..., None]
    lower = jnp.where(strict, jnp.einsum("bnhcd,bnhsd->bnhcs", kb, k) * decay, 0.0)
    tmat = jnp.eye(CHUNK, dtype=jnp.float32) + lower
    rhs = jnp.concatenate([v * beta[..., None], kb * jnp.exp(g_cum)[..., None]], axis=-1)
    sol = lax.linalg.triangular_solve(tmat, rhs, left_side=True, lower=True)
    u, w = sol[..., :dv], sol[..., dv:]

    attn = jnp.where(causal, jnp.einsum("bnhcd,bnhsd->bnhcs", q, k) * decay, 0.0)
    qg = q * jnp.exp(g_cum)[..., None]
    g_last = g_cum[..., -1]
    k_dec = k * jnp.exp(g_last[..., None] - g_cum)[..., None]
    a_last = jnp.exp(g_last)

    def step(s, inp):
        qg_c, kd_c, u_c, w_c, at_c, al_c = inp
        v_new = u_c - jnp.einsum("bhck,bhkv->bhcv", w_c, s)
        o_c = jnp.einsum("bhck,bhkv->bhcv", qg_c, s) + jnp.einsum("bhcs,bhsv->bhcv", at_c, v_new)
        s = s * al_c[..., None, None] + jnp.einsum("bhck,bhcv->bhkv", kd_c, v_new)
        return s, o_c

    xs = tuple(jnp.moveaxis(a, 1, 0) for a in (qg, k_dec, u, w, attn, a_last))
    s0 = jnp.zeros((bsz, h, dk, dv), jnp.float32)
    _, o = lax.scan(step, s0, xs)
    o = jnp.transpose(o, (1, 0, 3, 2, 4))
    return o.reshape(bsz, t, h, dv)


def _s5(u, a_re, a_im, b_re, b_im, c_re, c_im, d, log_dt):
    f = jnp.float32
    u = u.astype(f)
    a_re, a_im = a_re.astype(f), a_im.astype(f)
    b_re, b_im = b_re.astype(f), b_im.astype(f)
    dt = jnp.exp(log_dt.astype(f))[:, None]
    mag = jnp.exp(a_re * dt)
    ab_re, ab_im = mag * jnp.cos(a_im * dt), mag * jnp.sin(a_im * dt)
    den = a_re * a_re + a_im * a_im
    nr, ni = ab_re - 1.0, ab_im
    co_re = (nr * a_re + ni * a_im) / den
    co_im = (ni * a_re - nr * a_im) / den
    bb_re = co_re[..., None] * b_re - co_im[..., None] * b_im
    bb_im = co_re[..., None] * b_im + co_im[..., None] * b_re
    bu_re = jnp.einsum("btgc,gpc->tbgp", u, bb_re)
    bu_im = jnp.einsum("btgc,gpc->tbgp", u, bb_im)
    t = u.shape[1]
    la_re = jnp.broadcast_to(ab_re, (t, 1) + ab_re.shape)
    la_im = jnp.broadcast_to(ab_im, (t, 1) + ab_im.shape)

    def combine(e1, e2):
        a1r, a1i, b1r, b1i = e1
        a2r, a2i, b2r, b2i = e2
        return (a1r * a2r - a1i * a2i,
                a1r * a2i + a1i * a2r,
                a2r * b1r - a2i * b1i + b2r,
                a2r * b1i + a2i * b1r + b2i)

    _, _, x_re, x_im = lax.associative_scan(combine, (la_re, la_im, bu_re, bu_im), axis=0)
    y = (jnp.einsum("tbgp,gcp->btgc", x_re, c_re.astype(f))
         - jnp.einsum("tbgp,gcp->btgc", x_im, c_im.astype(f)))
    return y + d.astype(f) * u


def _mixer(xn, w_in, conv_w, dn_a_log, dn_dt_bias, dn_norm_g,
           s5_a_re, s5_a_im, s5_b_re, s5_b_im, s5_c_re, s5_c_im, s5_d, s5_log_dt,
           s5_w_glu, s5_b_glu, w_out):
    bsz, t, _ = xn.shape
    dt_ = xn.dtype
    proj = xn @ w_in

    qkv = jax.nn.silu(_causal_dwconv(proj[..., OFF_Q:OFF_Z], conv_w)).astype(jnp.float32)
    q = qkv[..., :DN_WIDTH].reshape(bsz, t, DN_HEADS, DN_HEAD_DIM)
    k = qkv[..., DN_WIDTH:2 * DN_WIDTH].reshape(bsz, t, DN_HEADS, DN_HEAD_DIM)
    v = qkv[..., 2 * DN_WIDTH:].reshape(bsz, t, DN_HEADS, DN_HEAD_DIM)
    q = _l2norm(q) * (DN_HEAD_DIM ** -0.5)
    k = _l2norm(k)
    z = proj[..., OFF_Z:OFF_A].astype(jnp.float32).reshape(bsz, t, DN_HEADS, DN_HEAD_DIM)
    a = proj[..., OFF_A:OFF_B].astype(jnp.float32)
    b = proj[..., OFF_B:OFF_U].astype(jnp.float32)
    g = -jnp.exp(dn_a_log.astype(jnp.float32)) * jax.nn.softplus(a + dn_dt_bias.astype(jnp.float32))
    beta = jax.nn.sigmoid(b)
    o = _gated_delta_rule(q, k, v, g, beta)
    o = o * lax.rsqrt(jnp.mean(o * o, axis=-1, keepdims=True) + EPS)
    o = o * dn_norm_g.astype(jnp.float32) * jax.nn.silu(z)
    o = o.reshape(bsz, t, DN_WIDTH).astype(dt_)

    u = proj[..., OFF_U:].reshape(bsz, t, S5_GROUPS, S5_CH_PER_GROUP)
    y = _s5(u, s5_a_re, s5_a_im, s5_b_re, s5_b_im, s5_c_re, s5_c_im, s5_d, s5_log_dt)
    y = jax.nn.gelu(y.reshape(bsz, t, S5_WIDTH)).astype(dt_)
    y = y * jax.nn.sigmoid(y @ s5_w_glu + s5_b_glu)

    return jnp.concatenate([o, y], axis=-1) @ w_out


def _cross_attn(hn, mem, norm_mem_g, w_xq, w_xk, w_xv, w_xo):
    bsz, t, _ = hn.shape
    mn = _rmsnorm(mem, norm_mem_g)
    q = (hn @ w_xq).reshape(bsz, t, X_HEADS, X_HEAD_DIM)
    k = (mn @ w_xk).reshape(bsz, N_MEM, X_HEADS, X_HEAD_DIM)
    v = (mn @ w_xv).reshape(bsz, N_MEM, X_HEADS, X_HEAD_DIM)
    s = jnp.einsum("bthd,bmhd->bhtm", q, k).astype(jnp.float32) * (X_HEAD_DIM ** -0.5)
    p = jax.nn.softmax(s, axis=-1).astype(hn.dtype)
    o = jnp.einsum("bhtm,bmhd->bthd", p, v).reshape(bsz, t, D_MODEL)
    return o @ w_xo


def _swiglu(hn, w_gate, w_up, w_down):
    return (jax.nn.silu(hn @ w_gate) * (hn @ w_up)) @ w_down


def setup_inputs(seed: int = 0) -> dict:
    key = jax.random.key(seed)
    ks = iter(jax.random.split(key, 40))
    nrm = lambda shape, scale: jax.random.normal(next(ks), shape, jnp.float32) * scale
    unif = lambda shape, lo, hi: jax.random.uniform(next(ks), shape, jnp.float32, lo, hi)
    L = DEPTH
    x = nrm((BATCH, SEQ, D_MODEL), 1.0)
    mem = nrm((BATCH, N_MEM, D_MODEL), 1.0)
    gain = lambda n: 1.0 + nrm((L, n), 0.02)
    dt_dn = jnp.exp(unif((L, DN_HEADS), math.log(1e-3), math.log(1e-1)))
    n_idx = jnp.arange(S5_STATE, dtype=jnp.float32)
    return {
        "x": x,
        "mem": mem,
        "norm_mix_g": gain(D_MODEL),
        "w_in": nrm((L, D_MODEL, D_IN), D_MODEL ** -0.5),
        "conv_w": nrm((L, CONV_WIDTH, 3 * DN_WIDTH), CONV_WIDTH ** -0.5),
        "dn_a_log": jnp.log(unif((L, DN_HEADS), 1.0, 16.0)),
        "dn_dt_bias": dt_dn + jnp.log(-jnp.expm1(-dt_dn)),
        "dn_norm_g": gain(DN_HEAD_DIM),
        "s5_a_re": -0.5 + nrm((L, S5_GROUPS, S5_STATE), 0.01),
        "s5_a_im": math.pi * n_idx + nrm((L, S5_GROUPS, S5_STATE), 0.01),
        "s5_b_re": nrm((L, S5_GROUPS, S5_STATE, S5_CH_PER_GROUP), (2 * S5_CH_PER_GROUP) ** -0.5),
        "s5_b_im": nrm((L, S5_GROUPS, S5_STATE, S5_CH_PER_GROUP), (2 * S5_CH_PER_GROUP) ** -0.5),
        "s5_c_re": nrm((L, S5_GROUPS, S5_CH_PER_GROUP, S5_STATE), (2 * S5_STATE) ** -0.5),
        "s5_c_im": nrm((L, S5_GROUPS, S5_CH_PER_GROUP, S5_STATE), (2 * S5_STATE) ** -0.5),
        "s5_d": nrm((L, S5_GROUPS, S5_CH_PER_GROUP), 1.0),
        "s5_log_dt": unif((L, S5_GROUPS), math.log(1e-3), math.log(1e-1)),
        "s5_w_glu": nrm((L, S5_WIDTH, S5_WIDTH), S5_WIDTH ** -0.5),
        "s5_b_glu": nrm((L, S5_WIDTH), 0.01),
        "w_out": nrm((L, D_MIX, D_MODEL), D_MIX ** -0.5),
        "norm_x_g": gain(D_MODEL),
        "norm_mem_g": gain(D_MODEL),
        "w_xq": nrm((L, D_MODEL, D_MODEL), D_MODEL ** -0.5),
        "w_xk": nrm((L, D_MODEL, D_MODEL), D_MODEL ** -0.5),
        "w_xv": nrm((L, D_MODEL, D_MODEL), D_MODEL ** -0.5),
        "w_xo": nrm((L, D_MODEL, D_MODEL), D_MODEL ** -0.5),
        "norm_ffn_g": gain(D_MODEL),
        "w_gate": nrm((L, D_MODEL, D_FF), D_MODEL ** -0.5),
        "w_up": nrm((L, D_MODEL, D_FF), D_MODEL ** -0.5),
        "w_down": nrm((L, D_FF, D_MODEL), D_FF ** -0.5),
        "norm_final_g": 1.0 + nrm((D_MODEL,), 0.02),
    }


def reference(x, mem, norm_mix_g, w_in, conv_w, dn_a_log, dn_dt_bias, dn_norm_g,
              s5_a_re, s5_a_im, s5_b_re, s5_b_im, s5_c_re, s5_c_im, s5_d, s5_log_dt,
              s5_w_glu, s5_b_glu, w_out, norm_x_g, norm_mem_g, w_xq, w_xk, w_xv, w_xo,
              norm_ffn_g, w_gate, w_up, w_down, norm_final_g):
    h = x
    for l in range(DEPTH):
        h = h + _mixer(_rmsnorm(h, norm_mix_g[l]), w_in[l], conv_w[l], dn_a_log[l],
                       dn_dt_bias[l], dn_norm_g[l], s5_a_re[l], s5_a_im[l], s5_b_re[l],
                       s5_b_im[l], s5_c_re[l], s5_c_im[l], s5_d[l], s5_log_dt[l],
                       s5_w_glu[l], s5_b_glu[l], w_out[l])
        h = h + _cross_attn(_rmsnorm(h, norm_x_g[l]), mem, norm_mem_g[l],
                            w_xq[l], w_xk[l], w_xv[l], w_xo[l])
        h = h + _swiglu(_rmsnorm(h, norm_ffn_g[l]), w_gate[l], w_up[l], w_down[l])
    return _rmsnorm(h, norm_final_g)
```

```python
import contextlib
import math
import numpy as np
import concourse.bass as bass
import concourse.mybir as mybir
from concourse.bass_utils import run_bass_kernel_spmd

F32 = mybir.dt.float32
BF16 = mybir.dt.bfloat16
AF = mybir.ActivationFunctionType
ALU = mybir.AluOpType

D = 1024
T_OWN = 2048
T_EXT = 4096
NMEM = 256
DIN = 2568
OFF_Q, OFF_K, OFF_V, OFF_Z, OFF_A, OFF_B, OFF_U = 0, 512, 1024, 1536, 2048, 2052, 2056
DFF = 2816
EPS = 1e-6
PI = math.pi
NEG = -30000.0


class Buf:
    __slots__ = ("name", "w", "r", "dsem", "dcnt", "t", "lock")

    def __init__(self, name, t=None, lock=None):
        self.name = name
        self.lock = lock
        self.w = None
        self.r = {}
        self.dsem = None
        self.dcnt = 0
        self.t = t

    def __getitem__(self, k):
        return self.t[k]


class Sched:
    ENG = ("pe", "act", "dve", "pool", "sp")

    def __init__(self, nc, es):
        self.nc = nc
        self.es = es
        self.q = {e: [] for e in self.ENG}
        self.cnt = {e: 0 for e in self.ENG}
        self.sems = {}
        self.waited = {}
        self.nbuf = 0
        self.dbufs = []
        self.EP = 2000

    def _etok(self, eng, cnt):
        key = "%s#%d" % (eng, (cnt - 1) // self.EP)
        if key not in self.sems:
            self.sems[key] = self.es.enter_context(self.nc.semaphore("s_%s_%d" % (eng, (cnt - 1) // self.EP)))
        return (key, (cnt - 1) % self.EP + 1)

    def _dsem(self, b):
        if b.dsem is None:
            key = "d%d" % self.nbuf
            self.nbuf += 1
            self.sems[key] = self.es.enter_context(self.nc.semaphore(key))
            b.dsem = key
            self.dbufs.append(b)
        return b.dsem

    def _waits(self, eng, toks):
        out = []
        for tk in toks:
            if tk is None:
                continue
            key, val = tk
            if eng == "pe" and key.startswith("pe#"):
                continue
            if self.waited.get((eng, key), 0) >= val:
                continue
            self.waited[(eng, key)] = val
            out.append((key, val))
        return out

    def op(self, eng, fn, reads=(), writes=()):
        locks = {}
        for b in list(reads) + list(writes):
            if b.lock is not None:
                locks[id(b.lock)] = b.lock
        if locks:
            writes = list(writes) + list(locks.values())
        toks = []
        for b in reads:
            toks.append(b.w)
        pre = eng + "#"
        for b in writes:
            toks.append(b.w)
            toks.extend(b.r.items())
        waits = self._waits(eng, toks)
        self.cnt[eng] += 1
        tok = self._etok(eng, self.cnt[eng])
        self.q[eng].append((waits, fn, (tok[0], 1)))
        for b in reads:
            b.r[tok[0]] = tok[1]
        for b in writes:
            b.w = tok
            b.r = {}
        return tok

    def dma(self, eng, out, in_, reads=(), writes=(), slow=False):
        toks = []
        for b in reads:
            toks.append(b.w)
        for b in writes:
            if b.w is not None and b.w[0] != b.dsem:
                toks.append(b.w)
            toks.extend(b.r.items())
        waits = self._waits(eng, toks)
        tb = (list(writes) + list(reads))[0]
        key = self._dsem(tb)
        tb.dcnt += 1
        tok = (key, 16 * tb.dcnt)
        if slow:
            self.q[eng].append((waits, lambda e: e.dma_start(out=out, in_=in_, allow_slow_non_contiguous=True), (key, 16)))
        else:
            self.q[eng].append((waits, lambda e: e.dma_start(out=out, in_=in_), (key, 16)))
        for b in reads:
            b.r[tok[0]] = tok[1]
        for b in writes:
            b.w = tok
            b.r = {}
        return tok

    def barrier(self):
        toks = [self._etok(e, self.cnt[e]) for e in self.ENG if self.cnt[e] > 0]
        for b in self.dbufs:
            toks.append((b.dsem, 16 * b.dcnt))
        for e in self.ENG:
            w = self._waits(e, toks)
            if w:
                self.q[e].append((w, None, None))

    def finish(self, final_bufs):
        toks = []
        for b in final_bufs:
            toks.extend(b.r.items())
            toks.append(b.w)
        fw = self._waits("sp", toks)
        nc = self.nc
        engmap = {"pe": "tensor", "act": "scalar", "dve": "vector", "pool": "gpsimd", "sp": "sync"}
        with nc.Block() as block:
            for e in self.ENG:
                q = self.q[e]
                extra = fw if e == "sp" else []

                def body(engine, q=q, extra=extra):
                    for waits, fn, inc in q:
                        for (wk, wv) in waits:
                            engine.wait_ge(self.sems[wk], wv)
                        if fn is not None:
                            ins = fn(engine)
                            ins.then_inc(self.sems[inc[0]], inc[1])
                    for (wk, wv) in extra:
                        engine.wait_ge(self.sems[wk], wv)

                getattr(block, engmap[e])(body)

    def tt(self, eng, out, in0, in1, op, R, W):
        return self.op(eng, lambda e: e.tensor_tensor(out=out, in0=in0, in1=in1, op=op), R, W)

    def ts(self, eng, out, in0, s1, s2, op0, op1, R, W):
        if op1 is None:
            return self.op(eng, lambda e: e.tensor_scalar(out=out, in0=in0, scalar1=s1, scalar2=None, op0=op0), R, W)
        return self.op(eng, lambda e: e.tensor_scalar(out=out, in0=in0, scalar1=s1, scalar2=s2, op0=op0, op1=op1), R, W)

    def stt(self, eng, out, in0, scalar, in1, op0, op1, R, W):
        return self.op(eng, lambda e: e.scalar_tensor_tensor(out=out, in0=in0, scalar=scalar, in1=in1, op0=op0, op1=op1), R, W)

    def act(self, out, in_, func, R, W, bias=None, scale=None, accum=None):
        kw = {}
        if bias is not None:
            kw["bias"] = bias
        if scale is not None:
            kw["scale"] = scale
        if accum is not None:
            kw["accum_out"] = accum
        return self.op("act", lambda e: e.activation(out=out, in_=in_, func=func, **kw), R, W)

    def copy(self, eng, out, in_, R, W):
        if eng == "act":
            return self.act(out, in_, AF.Copy, R, W)
        return self.op(eng, lambda e: e.tensor_copy(out=out, in_=in_), R, W)

    def memset(self, eng, ap, val, W):
        return self.op(eng, lambda e: e.memset(ap, val), (), W)

    def mm(self, out, lhsT, rhs, start, stop, R, W):
        return self.op("pe", lambda e: e.matmul(out, lhsT=lhsT, rhs=rhs, start=start, stop=stop), R, W)

    def tr(self, out, in_, ident, R, W):
        return self.op("pe", lambda e: e.transpose(out=out, in_=in_, identity=ident), R, W)

    def scan(self, out, d0, d1, init, R, W):
        return self.op("dve", lambda e: e.tensor_tensor_scan(out=out, data0=d0, data1=d1, initial=init,
                                                            op0=ALU.mult, op1=ALU.add), R, W)


class Arena:
    def __init__(self, t, nwords):
        self.t = t
        self.n = nwords
        self.off = 0

    def alloc(self, name, shape, dtype=F32):
        nel = 1
        for s in shape[1:]:
            nel *= s
        words = nel if dtype == F32 else (nel + 1) // 2
        words = (words + 1) // 2 * 2
        ap = self.t[:, self.off:self.off + words]
        self.off += words
        assert self.off <= self.n, (name, self.off, self.n)
        if dtype != F32:
            ap = ap.bitcast(dtype)
        ap = ap[:, 0:nel]
        if len(shape) == 3:
            ap = ap.rearrange("p (a b) -> p a b", a=shape[1])
        elif len(shape) == 4:
            ap = ap.rearrange("p (a b c) -> p a b c", a=shape[1], b=shape[2])
        if shape[0] < 128:
            ap = ap[0:shape[0]]
        return Buf(name, ap)


def bc(ap, shape):
    return ap.to_broadcast(list(shape))


class Prog:
    def __init__(self, taps=(), stop_after=None):
        self.taps = set(taps)
        self.stop_after = stop_after
        nc = bass.Bass("TRN2", target_bir_lowering=False)
        self.nc = nc
        dt = nc.dram_tensor
        I = {}

        def din(name, shape):
            I[name] = dt(name, list(shape), F32, kind="ExternalInput").ap()

        din("xe", [T_EXT, D])
        din("mem", [NMEM, D])
        din("norm_mix_g", [1, D])
        din("w_in", [1, D, DIN])
        din("conv_w", [1, 4, 1536])
        din("dn_a_log", [1, 4])
        din("dn_dt_bias", [1, 4])
        din("dn_norm_g", [1, 128])
        din("s5_a_re", [1, 32, 64])
        din("s5_a_im", [1, 32, 64])
        din("s5_b_re", [1, 32, 64, 16])
        din("s5_b_im", [1, 32, 64, 16])
        din("s5_c_re", [1, 32, 16, 64])
        din("s5_c_im", [1, 32, 16, 64])
        din("s5_d", [1, 32, 16])
        din("s5_log_dt", [1, 32])
        din("s5_w_glu", [1, 512, 512])
        din("s5_b_glu", [1, 512])
        din("w_out", [1, D, D])
        din("norm_x_g", [1, D])
        din("norm_mem_g", [1, D])
        din("w_xq", [1, D, D])
        din("w_xk", [1, D, D])
        din("w_xv", [1, D, D])
        din("w_xo", [1, D, D])
        din("norm_ffn_g", [1, D])
        din("w_gate", [1, D, DFF])
        din("w_up", [1, D, DFF])
        din("w_down", [1, DFF, D])
        din("norm_final_g", [D])
        self.I = I
        self.y = dt("y", [T_OWN, D], F32, kind="ExternalOutput").ap()
        self.tapd = {}
        if "oy" in self.taps:
            self.tapd["oy"] = dt("tap_oy", [128, 8, T_OWN], BF16, kind="ExternalOutput").ap()
        if "h" in self.taps:
            self.tapd["h"] = dt("tap_h", [T_OWN, D], F32, kind="ExternalOutput").ap()
        if "dbg" in self.taps:
            self.tapd["dbg"] = dt("tap_dbg", [128, 4096], F32, kind="ExternalOutput").ap()

        with contextlib.ExitStack() as es:
            self.es = es
            S = Sched(nc, es)
            self.S = S
            AW = 46 * 1024
            arena_t = es.enter_context(nc.sbuf_tensor("arena", [128, AW], F32))
            self.A = Arena(arena_t, AW)
            self.psum = es.enter_context(nc.psum_tensor("psum", [128, 4096], F32))
            self.plocks = [Buf("plock%d" % i) for i in range(8)]
            self.final = []
            self.consts()
            self.build()
            S.finish(self.final)

    def cbuf(self, name, shape, dtype=F32):
        t = self.es.enter_context(self.nc.sbuf_tensor(name, list(shape), dtype))
        return Buf(name, t)

    def consts(self):
        S = self.S
        self.identf = self.cbuf("identf", [128, 128])
        self.identb = self.cbuf("identb", [128, 128], BF16)
        self.onesb = self.cbuf("onesb", [128, 128], BF16)
        self.onesf = self.cbuf("onesf", [128, 128])
        self.zerosf = self.cbuf("zerosf", [128, 128])
        self.negmask = self.cbuf("negmask", [128, 128])
        self.strictm = self.cbuf("strictm", [128, 128])
        self.sel = self.cbuf("sel", [128, 4, 128])
        self.resetm = self.cbuf("resetm", [128, 256])
        onesf, zerosf, identf = self.onesf, self.zerosf, self.identf
        S.memset("pool", onesf[:], 1.0, [onesf])
        S.memset("pool", zerosf[:], 0.0, [zerosf])
        S.memset("dve", self.onesb[:], 1.0, [self.onesb])

        def asel(out, in_, cmp, fill, base, cm, pattern, R, W):
            S.op("pool", lambda e: e.affine_select(out=out, in_=in_, pattern=pattern, compare_op=cmp, fill=fill,
                                                   base=base, channel_multiplier=cm), R, W)

        asel(identf[:], onesf[:], ALU.is_equal, 0.0, 0, -1, [[1, 128]], [onesf], [identf])
        S.copy("dve", self.identb[:], identf[:], [identf], [self.identb])
        asel(self.negmask[:], zerosf[:], ALU.is_ge, NEG, 0, -1, [[1, 128]], [zerosf], [self.negmask])
        S.memset("pool", self.negmask[0:64, 64:128], NEG, [self.negmask])
        asel(self.strictm[:], onesf[:], ALU.is_ge, 0.0, -1, -1, [[1, 128]], [onesf], [self.strictm])
        S.memset("pool", self.strictm[0:64, 64:128], 0.0, [self.strictm])
        for h in range(4):
            asel(self.sel[:, h, :], onesf[:], ALU.is_equal, 0.0, -h, 1, [[0, 128]], [onesf], [self.sel])
        S.memset("dve", self.resetm[:], 1.0, [self.resetm])
        for c in range(4):
            S.memset("dve", self.resetm[:, c * 64:c * 64 + 1], 0.0, [self.resetm])

    def bank(self, i, name=None):
        return Buf(name or ("bank%d" % i), self.psum[:, i * 512:(i + 1) * 512], lock=self.plocks[i])

    def load_w_bf16(self, dst, src, nk, chunk_cols=None):
        for k in range(nk):
            self.S.dma("pool", dst[:, k, :], src[k * 128:(k + 1) * 128, :], writes=[dst])

    def rms_stats(self, xap, xbuf, junk, ss, rs, nfeat=1024):
        S = self.S
        S.act(junk[:], xap, AF.Square, [xbuf], [junk, ss], accum=ss[:])
        S.act(rs[:], ss[:], AF.Ln, [ss], [rs], scale=1.0 / nfeat, bias=self.epsc[:, 0:1])
        S.act(rs[:], rs[:], AF.Exp, [rs], [rs], scale=-0.5)

    def norm_to_T(self, xap, xbuf, gain, xnT, col0, tb, tmp):
        S = self.S
        junk, ss, rs, xn = tmp
        self.rms_stats(xap, xbuf, junk, ss, rs)
        S.stt("dve", xn[:], xap, rs[:, 0:1], gain[:], ALU.mult, ALU.mult, [xbuf, rs, gain], [xn])
        tv = tb.t.bitcast(BF16).rearrange("p (a b) -> p a b", a=8)
        for k in range(8):
            S.tr(tv[:, k, :], xn[:, k * 128:(k + 1) * 128], self.identb[:], [xn, self.identb], [tb])
        S.copy("act", xnT[:, :, col0:col0 + 128], tv, [tb], [xnT])

    def build(self):
        S, A, I = self.S, self.A, self.I
        self.epsc = self.cbuf("epsc", [128, 1])
        S.memset("dve", self.epsc[:], EPS, [self.epsc])
        oy_all = A.alloc("oy", [128, 8, T_OWN], BF16)
        self.OY = [Buf("oy%d" % b, oy_all.t[:, :, b * 512:(b + 1) * 512]) for b in range(4)]
        self.oy_all = oy_all
        mark = A.off
        import os
        if not os.environ.get("SKIP_S5"):
            self.sweep_s5()
            S.barrier()
        A.off = mark
        if self.stop_after == "s5":
            return self.finish_taps()
        self.sweep_dn()
        S.barrier()
        A.off = mark
        if self.stop_after == "dn":
            return self.finish_taps()
        self.phase_x0()
        S.barrier()
        self.phase_x1()
        S.barrier()
        if self.stop_after == "x":
            return self.finish_taps()
        self.phase_f()
        self.finish_taps()

    def finish_taps(self):
        S = self.S
        if "oy" in self.tapd:
            for b in range(4):
                S.dma("sp", self.tapd["oy"][:, :, b * 512:(b + 1) * 512], self.OY[b][:], reads=[self.OY[b]])
                self.final.append(self.OY[b])
        if "h" in self.tapd and hasattr(self, "H"):
            for i in range(16):
                S.dma("sp", self.tapd["h"][i * 128:(i + 1) * 128, :], self.H[i][:], reads=[self.H[i]])
                self.final.append(self.H[i])
        if "dbg" in self.tapd and hasattr(self, "dbg"):
            S.dma("sp", self.tapd["dbg"][:, 0:self.dbg.t.shape[-1]], self.dbg[:], reads=[self.dbg])
            self.final.append(self.dbg)
        if self.stop_after is not None:
            z = self.cbuf("zfill", [128, D])
            S.memset("dve", z[:], 0.0, [z])
            for i in range(16):
                S.dma("sp", self.y[i * 128:(i + 1) * 128, :], z[:], reads=[z])
            self.final.append(z)

    def cmul(self, eng, o_re, o_im, a_re, a_im, b_re, b_im, t1, t2, R, W):
        S = self.S
        S.tt(eng, t1, a_re, b_re, ALU.mult, R, W)
        S.tt(eng, t2, a_im, b_im, ALU.mult, R, W)
        S.tt(eng, o_re, t1, t2, ALU.subtract, R, W)
        S.tt(eng, t1, a_re, b_im, ALU.mult, R, W)
        S.tt(eng, t2, a_im, b_re, ALU.mult, R, W)
        S.tt(eng, o_im, t1, t2, ALU.add, R, W)

    def sweep_s5(self):
        S, A, I = self.S, self.A, self.I
        wu = A.alloc("wu", [128, 8, 512], BF16)
        for k in range(8):
            S.dma("pool", wu[:, k, :], I["w_in"][0, k * 128:(k + 1) * 128, OFF_U:OFF_U + 512], writes=[wu])
        wglu = A.alloc("wglu", [128, 4, 512], BF16)
        for k in range(4):
            S.dma("pool", wglu[:, k, :], I["s5_w_glu"][0, k * 128:(k + 1) * 128, :], writes=[wglu])
        prm = A.alloc("s5prm", [128, 64, 16])
        P = lambda i: prm.t[:, i, :]
        bglu = A.alloc("bglu", [128, 4])
        S.dma("sp", bglu[:], I["s5_b_glu"][0].rearrange("(f p) -> p f", p=128), writes=[bglu], slow=True)
        dcol = A.alloc("dcol", [128, 4])
        S.dma("sp", dcol[:], I["s5_d"][0].rearrange("(j g) c -> (g c) j", g=8), writes=[dcol], slow=True)
        gain = A.alloc("gmix", [128, 1024])
        S.dma("sp", gain[:], I["norm_mix_g"][0:1, :].partition_broadcast(128), writes=[gain])
        ARE, AIM, LDT, DT, MAG, TH, S1, C1, NR, DEN, RDEN, CORE, COIM, T1, T2, T3, ABR, ABI, NTH, HPI = range(20)
        ER = [20 + k for k in range(7)]
        EI = [27 + k for k in range(7)]
        IR = [34, 35]
        II = [36, 37]
        TI = [38, 39, 40, 41]
        S.dma("sp", P(ARE), I["s5_a_re"][0].rearrange("(q t) p -> (t p) q", t=2), writes=[prm], slow=True)
        S.dma("sp", P(AIM), I["s5_a_im"][0].rearrange("(q t) p -> (t p) q", t=2), writes=[prm], slow=True)
        ldv = I["s5_log_dt"][0].rearrange("(q t) -> t q", t=2)
        for t in range(2):
            S.dma("sp", prm.t[t * 64:(t + 1) * 64, LDT, :], ldv[t].partition_broadcast(64), writes=[prm], slow=True)
        TC = A.alloc("TC", [128, 16, 64])
        TSn = A.alloc("TS", [128, 16, 64])
        BL = A.alloc("BL", [128, 16, 2, 128], BF16)
        CL = A.alloc("CL", [128, 16, 2, 128], BF16)
        diagD = A.alloc("diagD", [128, 4, 128], BF16)
        xs = [A.alloc("xs%d" % i, [128, 1024]) for i in range(2)]
        junk = A.alloc("junk", [128, 1024], BF16)
        ss = A.alloc("ss", [128, 1])
        rs = A.alloc("rs", [128, 1])
        xn = A.alloc("xn", [128, 1024], BF16)
        xnT = A.alloc("xnT", [128, 8, 512], BF16)
        uT = [A.alloc("uT%d" % i, [128, 4, 512], BF16) for i in range(2)]
        mark_setup = A.off
        tA = A.alloc("tA", [128, 16, 64])
        tB = A.alloc("tB", [128, 16, 64])
        BPr = A.alloc("BPr", [128, 16, 128])
        BPi = A.alloc("BPi", [128, 16, 128])
        BBr = A.alloc("BBr", [128, 16, 128])
        BBi = A.alloc("BBi", [128, 16, 128])
        tC = A.alloc("tC", [128, 16, 128])
        tD = A.alloc("tD", [128, 16, 128])
        CPr = A.alloc("CPr", [128, 16, 128])
        CPi = A.alloc("CPi", [128, 16, 128])
        for b_ in (BPr, BPi, CPr, CPi):
            S.memset("pool", b_[:], 0.0, [b_])
        for g in range(32):
            q, t = g // 2, g % 2
            c0 = (g % 8) * 16
            S.dma("sp", BPr[t * 64:(t + 1) * 64, q, c0:c0 + 16], I["s5_b_re"][0, g], writes=[BPr])
            S.dma("act", BPi[t * 64:(t + 1) * 64, q, c0:c0 + 16], I["s5_b_im"][0, g], writes=[BPi])
            S.dma("sp", CPr[c0:c0 + 16, q, t * 64:(t + 1) * 64], I["s5_c_re"][0, g], writes=[CPr])
            S.dma("act", CPi[c0:c0 + 16, q, t * 64:(t + 1) * 64], I["s5_c_im"][0, g], writes=[CPi])
        pb = [self.bank(6, "s5b6"), self.bank(7, "s5b7")]
        tb = self.bank(5, "s5tb")
        def front(b):
            for i in range(4):
                x_ = xs[(b * 4 + i) % 2]
                S.dma("sp", x_[:], I["xe"][b * 512 + i * 128:b * 512 + (i + 1) * 128, :], writes=[x_])
                self.norm_to_T(x_[:], x_, gain, xnT, i * 128, tb, (junk, ss, rs, xn))
            uu_ = uT[b % 2]
            for j in range(4):
                pbk = pb[j % 2]
                for k in range(8):
                    S.mm(pbk[:], wu[:, k, j * 128:(j + 1) * 128], xnT[:, k, :], k == 0, k == 7, [wu, xnT], [pbk])
                S.copy("act", uu_[:, j, :], pbk[:], [pbk], [uu_])

        front(0)
        R1 = [prm]
        S.act(P(DT), P(LDT), AF.Exp, R1, R1)
        S.tt("dve", P(T1), P(ARE), P(DT), ALU.mult, R1, R1)
        S.act(P(MAG), P(T1), AF.Exp, R1, R1)
        S.tt("dve", P(TH), P(AIM), P(DT), ALU.mult, R1, R1)
        for _ in range(5):
            S.ts("dve", P(T1), P(TH), PI, -2.0 * PI, ALU.is_gt, ALU.mult, R1, R1)
            S.tt("dve", P(TH), P(TH), P(T1), ALU.add, R1, R1)
        S.act(P(S1), P(TH), AF.Sin, R1, R1)
        S.ts("dve", P(NTH), P(TH), -1.0, None, ALU.mult, None, R1, R1)
        S.tt("dve", P(NTH), P(NTH), P(TH), ALU.max, R1, R1)
        S.memset("dve", P(HPI), PI / 2, R1)
        S.act(P(C1), P(NTH), AF.Sin, R1, R1, scale=-1.0, bias=prm.t[:, HPI, 0:1])
        S.tt("dve", P(ABR), P(MAG), P(C1), ALU.mult, R1, R1)
        S.tt("dve", P(ABI), P(MAG), P(S1), ALU.mult, R1, R1)
        S.ts("dve", P(NR), P(ABR), -1.0, None, ALU.add, None, R1, R1)
        S.tt("dve", P(T1), P(ARE), P(ARE), ALU.mult, R1, R1)
        S.tt("dve", P(T2), P(AIM), P(AIM), ALU.mult, R1, R1)
        S.tt("dve", P(DEN), P(T1), P(T2), ALU.add, R1, R1)
        S.op("dve", lambda e: e.reciprocal(out=P(RDEN), in_=P(DEN)), R1, R1)
        S.tt("dve", P(T1), P(NR), P(ARE), ALU.mult, R1, R1)
        S.tt("dve", P(T2), P(ABI), P(AIM), ALU.mult, R1, R1)
        S.tt("dve", P(T3), P(T1), P(T2), ALU.add, R1, R1)
        S.tt("dve", P(CORE), P(T3), P(RDEN), ALU.mult, R1, R1)
        S.tt("dve", P(T1), P(ABI), P(ARE), ALU.mult, R1, R1)
        S.tt("dve", P(T2), P(NR), P(AIM), ALU.mult, R1, R1)
        S.tt("dve", P(T3), P(T1), P(T2), ALU.subtract, R1, R1)
        S.tt("dve", P(COIM), P(T3), P(RDEN), ALU.mult, R1, R1)
        S.copy("dve", P(ER[0]), P(C1), R1, R1)
        S.copy("dve", P(EI[0]), P(S1), R1, R1)
        for k in range(1, 7):
            S.tt("dve", P(T1), P(ER[k - 1]), P(ER[k - 1]), ALU.mult, R1, R1)
            S.tt("dve", P(T2), P(EI[k - 1]), P(EI[k - 1]), ALU.mult, R1, R1)
            S.tt("dve", P(ER[k]), P(T1), P(T2), ALU.subtract, R1, R1)
            S.tt("dve", P(T1), P(ER[k - 1]), P(EI[k - 1]), ALU.mult, R1, R1)
            S.ts("dve", P(EI[k]), P(T1), 2.0, None, ALU.mult, None, R1, R1)
        S.memset("dve", TC[:, :, 0:1], 1.0, [TC])
        S.memset("dve", TSn[:, :, 0:1], 0.0, [TSn])
        S.copy("dve", TC[:, :, 1:2], P(ER[0]).unsqueeze(2), [prm], [TC])
        S.copy("dve", TSn[:, :, 1:2], P(EI[0]).unsqueeze(2), [prm], [TSn])
        for k in range(1, 6):
            n = 1 << k
            er = bc(P(ER[k]).unsqueeze(2), [128, 16, n])
            ei = bc(P(EI[k]).unsqueeze(2), [128, 16, n])
            self.cmul("dve", TC[:, :, n:2 * n], TSn[:, :, n:2 * n], TC[:, :, 0:n], TSn[:, :, 0:n], er, ei,
                      tA[:, :, 0:n], tB[:, :, 0:n], [prm, TC, TSn, tA, tB], [TC, TSn, tA, tB])
        cr = bc(P(CORE).unsqueeze(2), [128, 16, 128])
        ci = bc(P(COIM).unsqueeze(2), [128, 16, 128])
        self.cmul("dve", BBr[:], BBi[:], BPr[:], BPi[:], cr, ci, tC[:], tD[:], [prm, BPr, BPi, tC, tD, BBr, BBi], [BBr, BBi, tC, tD])
        n = 0
        for q in range(16):
            for (src, dst, part, sc) in ((BBr, BL, 0, None), (BBi, BL, 1, None), (CPr, CL, 0, None), (CPi, CL, 1, -1.0)):
                pbk = pb[n % 2]
                n += 1
                S.tr(pbk[:, 0:128], src[:, q, :], self.identf[:], [src, self.identf], [pbk])
                if sc is None:
                    S.copy("act", dst[:, q, part, :], pbk[:, 0:128], [pbk], [dst])
                else:
                    S.ts("dve", dst[:, q, part, :], pbk[:, 0:128], sc, None, ALU.mult, None, [pbk], [dst])
        for j in range(4):
            S.ts("dve", diagD[:, j, :], self.identf[:], dcol[:, j:j + 1], None, ALU.mult, None, [self.identf, dcol], [diagD])
        S.barrier()
        A.off = mark_setup

        bmr = A.alloc("bmr", [128, 16, 64])
        bmi = A.alloc("bmi", [128, 16, 64])
        xtr = [A.alloc("xtr%d" % i, [128, 16, 64]) for i in range(2)]
        xti = [A.alloc("xti%d" % i, [128, 16, 64]) for i in range(2)]
        Xb = [A.alloc("Xb%d" % i, [128, 2, 16, 64], BF16) for i in range(2)]
        m1 = A.alloc("m1", [128, 16, 64])
        m2 = A.alloc("m2", [128, 16, 64])
        RHOF = A.alloc("RHOF", [128, 16, 64])
        busR = A.alloc("busR", [128, 16, 64])
        busI = A.alloc("busI", [128, 16, 64])
        yraw = A.alloc("yraw", [128, 4, 512])
        g1 = A.alloc("g1", [128, 512])
        g2 = A.alloc("g2", [128, 512])
        ygb = A.alloc("ygb", [128, 4, 512], BF16)
        sg = [A.alloc("sg%d" % i, [128, 512]) for i in range(2)]
        bu_r = [Buf("bu_r%d" % h, self.psum[:, h * 512:(h + 1) * 512].rearrange("p (q j) -> p q j", q=8), lock=self.plocks[h]) for h in range(2)]
        bu_i = [Buf("bu_i%d" % h, self.psum[:, (2 + h) * 512:(3 + h) * 512].rearrange("p (q j) -> p q j", q=8), lock=self.plocks[2 + h]) for h in range(2)]
        ylock = self.plocks[4]
        yps = [Buf("yps%d" % h, self.psum[:, 4 * 512 + h * 256:4 * 512 + (h + 1) * 256].rearrange("p (a b) -> p a b", a=4), lock=ylock) for h in range(2)]
        IRb = [P(IR[0]), P(IR[1])]
        IIb = [P(II[0]), P(II[1])]
        S.memset("dve", IRb[0], 0.0, [prm])
        S.memset("dve", IIb[0], 0.0, [prm])
        S.copy("dve", RHOF[:], bc(P(MAG).unsqueeze(2), [128, 16, 64]), [prm], [RHOF])
        S.memset("dve", RHOF[:, :, 0:1], 0.0, [RHOF])
        S.tt("dve", P(TI[2]), P(ER[6]), P(MAG), ALU.mult, [prm], [prm])
        S.tt("dve", P(TI[3]), P(EI[6]), P(MAG), ALU.mult, [prm], [prm])
        nchunk = 0

        def bu(g):
            bb, cc = divmod(g, 8)
            uu_ = uT[bb % 2]
            c0_ = cc * 64
            for q in range(16):
                h_, ql = q // 8, q % 8
                S.mm(bu_r[h_][:, ql, :], BL[:, q, 0, :], uu_[:, q // 4, c0_:c0_ + 64], True, True, [BL, uu_], [bu_r[h_]])
                S.mm(bu_i[h_][:, ql, :], BL[:, q, 1, :], uu_[:, q // 4, c0_:c0_ + 64], True, True, [BL, uu_], [bu_i[h_]])
            for h_ in range(2):
                qs_ = slice(8 * h_, 8 * h_ + 8)
                S.copy("act", busR[:, qs_, :], bu_r[h_][:], [bu_r[h_]], [busR])
                S.copy("act", busI[:, qs_, :], bu_i[h_][:], [bu_i[h_]], [busI])

        bu(0)
        for b in range(8):
            own = b >= 4
            u_ = uT[b % 2]
            for ch in range(8):
                co = ch * 64
                cur = nchunk % 2
                nxt = (nchunk + 1) % 2
                S.tt("dve", bmr[:], busR[:], TC[:], ALU.mult, [busR, TC], [bmr])
                S.tt("dve", m1[:], busI[:], TSn[:], ALU.mult, [busI, TSn], [m1])
                S.tt("dve", bmi[:], busI[:], TC[:], ALU.mult, [busI, TC], [bmi])
                S.tt("dve", m2[:], busR[:], TSn[:], ALU.mult, [busR, TSn], [m2])
                S.tt("dve", bmr[:], bmr[:], m1[:], ALU.add, [bmr, m1], [bmr])
                S.tt("dve", bmi[:], bmi[:], m2[:], ALU.subtract, [bmi, m2], [bmi])
                S.tt("dve", bmr[:, :, 0], bmr[:, :, 0], IRb[cur], ALU.add, [bmr, prm], [bmr])
                S.tt("dve", bmi[:, :, 0], bmi[:, :, 0], IIb[cur], ALU.add, [bmi, prm], [bmi])
                if ch == 3 and b + 1 < 8:
                    front(b + 1)
                if nchunk + 1 < 64:
                    bu(nchunk + 1)
                xr, xi = xtr[cur], xti[cur]
                fl = lambda t_: t_.t.rearrange("p q j -> p (q j)")
                S.scan(fl(xr), fl(RHOF), fl(bmr), 0.0, [RHOF, bmr], [xr])
                S.scan(fl(xi), fl(RHOF), fl(bmi), 0.0, [RHOF, bmi], [xi])
                lr, li = xr[:, :, 63], xi[:, :, 63]
                R_, W_ = [prm, xr, xi], [prm]
                S.tt("dve", P(TI[0]), lr, P(TI[2]), ALU.mult, R_, W_)
                S.tt("dve", P(TI[1]), li, P(TI[3]), ALU.mult, R_, W_)
                S.tt("dve", P(42), lr, P(TI[3]), ALU.mult, R_, W_)
                S.tt("dve", P(43), li, P(TI[2]), ALU.mult, R_, W_)
                S.tt("dve", IRb[nxt], P(TI[0]), P(TI[1]), ALU.subtract, R_, W_)
                S.tt("dve", IIb[nxt], P(42), P(43), ALU.add, R_, W_)
                if own:
                    X_ = Xb[cur]
                    S.tt("dve", m1[:], TC[:], xr[:], ALU.mult, [TC, xr], [m1])
                    S.tt("dve", m2[:], TSn[:], xi[:], ALU.mult, [TSn, xi], [m2])
                    S.tt("dve", bmr[:], TSn[:], xr[:], ALU.mult, [TSn, xr], [bmr])
                    S.tt("dve", bmi[:], TC[:], xi[:], ALU.mult, [TC, xi], [bmi])
                    S.tt("dve", X_[:, 0, :, :], m1[:], m2[:], ALU.subtract, [m1, m2], [X_])
                    S.tt("dve", X_[:, 1, :, :], bmr[:], bmi[:], ALU.add, [bmr, bmi], [X_])
                    yp = yps[cur]
                    for j in range(4):
                        first = True
                        for q in range(4 * j, 4 * j + 4):
                            S.mm(yp[:, j, :], CL[:, q, 0, :], X_[:, 0, q, :], first, False, [CL, X_], [yp])
                            first = False
                            S.mm(yp[:, j, :], CL[:, q, 1, :], X_[:, 1, q, :], False, False, [CL, X_], [yp])
                        S.mm(yp[:, j, :], diagD[:, j, :], u_[:, j, co:co + 64], False, True, [diagD, u_], [yp])
                    S.copy("act", yraw[:, :, co:co + 64], yp[:], [yp], [yraw])
                nchunk += 1
            if own:
                ob = self.OY[b - 4]
                for j in range(4):
                    S.act(g1[:], yraw[:, j, :], AF.Square, [yraw], [g1])
                    S.ts("dve", g1[:], g1[:], 0.044715, 1.0, ALU.mult, ALU.add, [g1], [g1])
                    S.tt("pool", g2[:], g1[:], yraw[:, j, :], ALU.mult, [g1, yraw], [g2])
                    S.act(g1[:], g2[:], AF.Sigmoid, [g2], [g1], scale=1.5957691216)
                    S.tt("pool", ygb[:, j, :], yraw[:, j, :], g1[:], ALU.mult, [yraw, g1], [ygb])
                for f in range(4):
                    pbk = pb[f % 2]
                    for c in range(4):
                        S.mm(pbk[:], wglu[:, c, f * 128:(f + 1) * 128], ygb[:, c, :], c == 0, c == 3, [wglu, ygb], [pbk])
                    s_ = sg[f % 2]
                    S.act(s_[:], pbk[:], AF.Sigmoid, [pbk, bglu], [s_], bias=bglu[:, f:f + 1])
                    S.tt("pool", ob[:, 4 + f, :], ygb[:, f, :], s_[:], ALU.mult, [ygb, s_], [ob])

    def sweep_dn(self):
        S, A, I = self.S, self.A, self.I
        NB = 256
        wdn = A.alloc("wdn", [128, 8, 2304], BF16)
        wblk = {}
        for nm_, c0_, c1_ in (("g", 2048, 2304), ("k", 512, 1024), ("v", 1024, 1536), ("q", 0, 512), ("z", 1536, 2048)):
            wblk[nm_] = Buf("wdn_" + nm_, wdn.t[:, :, c0_:c1_])
        S.memset("pool", wdn[:, :, 2176:2304], 0.0, [wblk["g"]])
        for k in range(8):
            S.dma("pool", wdn[:, k, 2048:2176], I["w_in"][0, k * 128:(k + 1) * 128, 2048:2176], writes=[wblk["g"]])
            S.dma("pool", wdn[:, k, 2176:2180], I["w_in"][0, k * 128:(k + 1) * 128, OFF_B:OFF_B + 4], writes=[wblk["g"]])
        for nm_, c0_, c1_ in (("k", 512, 1024), ("v", 1024, 1536), ("q", 0, 512), ("z", 1536, 2048)):
            for k in range(8):
                S.dma("pool", wdn[:, k, c0_:c1_], I["w_in"][0, k * 128:(k + 1) * 128, c0_:c1_], writes=[wblk[nm_]])

        def wbuf(c0):
            return wblk["g"] if c0 >= 2048 else wblk["q"] if c0 < 512 else wblk["k"] if c0 < 1024 else wblk["v"] if c0 < 1536 else wblk["z"]
        gain = A.alloc("gmix", [128, 1024])
        S.dma("sp", gain[:], I["norm_mix_g"][0:1, :].partition_broadcast(128), writes=[gain])
        cw = A.alloc("convw", [128, 12, 4])
        for j in range(4):
            S.dma("sp", cw[:, :, j], I["conv_w"][0, j].rearrange("(f p) -> p f", p=128), writes=[cw], slow=True)
        sm = A.alloc("dnsm", [128, 8])
        S.memset("dve", sm[:], 0.0, [sm])
        S.dma("sp", sm[0:4, 0:1], I["dn_dt_bias"][0].rearrange("(a o) -> a o", o=1), writes=[sm])
        S.dma("sp", sm[0:4, 1:2], I["dn_a_log"][0].rearrange("(a o) -> a o", o=1), writes=[sm])
        S.dma("sp", sm[:, 2:3], I["dn_norm_g"][0].rearrange("(a o) -> a o", o=1), writes=[sm])
        S.act(sm[0:4, 1:2], sm[0:4, 1:2], AF.Exp, [sm], [sm])
        S.ts("dve", sm[0:4, 1:2], sm[0:4, 1:2], -1.0, None, ALU.mult, None, [sm], [sm])
        S.memset("dve", sm[:, 3:4], math.log(128 ** -0.5), [sm])
        S.memset("dve", sm[:, 4:5], 1.0, [sm])

        xs = A.alloc("xs", [128, 1024])
        ss = A.alloc("ss", [128, 1])
        rs = A.alloc("rs", [128, 1])
        xn = A.alloc("xn", [128, 1024], BF16)
        xnT = A.alloc("xnT", [128, 8, NB], BF16)
        raw = A.alloc("raw", [128, 12, NB + 4], BF16)
        S.memset("dve", raw[:], 0.0, [raw])
        acc = [A.alloc("acc%d" % i, [128, NB]) for i in range(2)]
        sqb = [A.alloc("sqb%d" % i, [128, NB], BF16) for i in range(2)]
        rn = [A.alloc("rn%d" % i, [128, NB]) for i in range(2)]
        qT2 = [A.alloc("qT%d" % i, [128, 4, NB], BF16) for i in range(2)]
        kT2 = [A.alloc("kT%d" % i, [128, 4, NB], BF16) for i in range(2)]
        vT2 = [A.alloc("vT%d" % i, [128, 4, NB], BF16) for i in range(2)]
        zs2 = [A.alloc("zs%d" % i, [128, 4, NB], BF16) for i in range(2)]
        qgT2 = [[A.alloc("qgT%d_%d" % (i, h), [128, NB], BF16) for h in range(4)] for i in range(2)]
        rows = A.alloc("rows", [128, 9, NB])
        S.memset("pool", rows[:], 0.0, [rows])
        R_TMP, R_G, R_GC, R_BETA, R_EGC, R_AL, R_NGC, R_BEGE, R_EKD = range(9)
        Rw = lambda i: rows.t[0:4, i, :]
        cols2 = [A.alloc("cols%d" % i, [128, 2, 4, 4]) for i in range(2)]
        GB2 = [A.alloc("GB%d" % i, [128, 4, 2, NB]) for i in range(2)]
        GCb2 = [Buf("GCbv%d" % i, GB2[i].t[:, :, 0, :]) for i in range(2)]
        BEb2 = [Buf("BEbv%d" % i, GB2[i].t[:, :, 1, :]) for i in range(2)]
        EGb = A.alloc("EGb", [128, 4, NB])
        ALc2 = [A.alloc("ALc%d" % i, [128, 4, 4]) for i in range(2)]
        d1 = [A.alloc("sd1_%d" % i, [128, 128]) for i in range(4)]
        dec = [A.alloc("sdec_%d" % i, [128, 128]) for i in range(4)]
        decs = [A.alloc("sdecs_%d" % i, [128, 128]) for i in range(4)]
        tbf = d1
        Pm = [A.alloc("sP%d" % i, [128, 128]) for i in range(4)]
        Um = [A.alloc("sU%d" % i, [128, 128]) for i in range(4)]
        Wm = [A.alloc("sW%d" % i, [128, 128]) for i in range(4)]
        Wbf = [A.alloc("Wbf%d" % i, [128, 128], BF16) for i in range(8)]
        RHSv = [A.alloc("RHSv%d" % i, [128, 128], BF16) for i in range(8)]
        RHSk = [A.alloc("RHSk%d" % i, [128, 128], BF16) for i in range(8)]
        kdec = [A.alloc("kdec%d" % i, [128, 128], BF16) for i in range(8)]
        uu = [A.alloc("uu%d" % i, [128, 128]) for i in range(8)]
        wT = [A.alloc("wT%d" % i, [128, 128], BF16) for i in range(8)]
        attnT = [A.alloc("attnT%d" % i, [128, 128], BF16) for i in range(8)]
        S32 = [A.alloc("S32_%d" % i, [128, 128]) for i in range(4)]
        Sbf = [[A.alloc("Sbf%d_%d" % (i, j), [128, 128], BF16) for j in range(3)] for i in range(4)]
        vnA = [A.alloc("vnA%d" % i, [128, 128], BF16) for i in range(4)]
        vnB = [A.alloc("vnB%d" % i, [128, 128], BF16) for i in range(4)]
        oraw = [A.alloc("oraw%d" % i, [128, NB]) for i in range(4)]
        for h in range(4):
            S.memset("dve", S32[h][:], 0.0, [S32[h]])
            S.memset("dve", Sbf[h][0][:], 0.0, [Sbf[h][0]])
            S.memset("pool", vnA[h][:], 0.0, [vnA[h]])
            S.memset("pool", vnB[h][:], 0.0, [vnB[h]])
        tb = self.bank(0, "dn_tb")
        pacc = [Buf("dn_pacc%d" % i, self.psum[:, (1 + i) * 512:(1 + i) * 512 + 256], lock=self.plocks[1 + i]) for i in range(2)]
        pn_ = Buf("dn_pnrm", self.psum[:, 3 * 512:3 * 512 + 256], lock=self.plocks[3])
        pnrm = [pn_, pn_]
        hlock = [self.plocks[4 + h] for h in range(4)]

        def slots(j, nm):
            return [Buf("%s%d" % (nm, h), self.psum[:, (4 + h) * 512 + j * 128:(4 + h) * 512 + (j + 1) * 128], lock=hlock[h]) for h in range(4)]
        pk, pu, pp, pw = slots(0, "pk"), slots(1, "pu"), slots(2, "pp"), slots(3, "pw")
        pq = pu
        identf, identb, sel = self.identf, self.identb, self.sel
        nacc = 0
        npair = 0
        FS = [dict(qT=qT2[i], kT=kT2[i], vT=vT2[i], zs=zs2[i], qgT=qgT2[i], GCb=GB2[i], BEb=GB2[i], cols=cols2[i], ALc=ALc2[i])
              for i in range(2)]
        GCv = {id(GB2[i]): GCb2[i].t for i in range(2)}
        BEv = {id(GB2[i]): BEb2[i].t for i in range(2)}

        def proj(c0):
            nonlocal nacc
            p_ = pacc[nacc % 2]
            nacc += 1
            wb_ = wbuf(c0)
            for k in range(8):
                S.mm(p_[:], wdn[:, k, c0:c0 + 128], xnT[:, k, :], k == 0, k == 7, [wb_, xnT], [p_])
            return p_

        def front(b, F):
            own = b >= 8
            qT, kT, vT, zs, qgT, GCb, BEb, cols, ALc = (F[n_] for n_ in ("qT", "kT", "vT", "zs", "qgT", "GCb", "BEb", "cols", "ALc"))
            for i in range(2):
                S.dma("sp", xs[:], I["xe"][b * NB + i * 128:b * NB + (i + 1) * 128, :], writes=[xs])
                self.norm_to_T(xs[:], xs, gain, xnT, i * 128, tb, (xn, ss, rs, xn))
                yield "any"
            p_ = proj(OFF_A)
            S.act(Rw(R_TMP), p_[0:4, :], AF.Exp, [p_, sm], [rows], bias=sm[0:4, 0:1])
            S.act(Rw(R_TMP), Rw(R_TMP), AF.Ln, [rows, sm], [rows], bias=sm[0:4, 4:5])
            S.ts("dve", Rw(R_G), Rw(R_TMP), sm[0:4, 1:2], None, ALU.mult, None, [rows, sm], [rows])
            yield "any"
            p_ = proj(2176)
            S.act(Rw(R_BETA), p_[0:4, :], AF.Exp, [p_], [rows], scale=-1.0)
            S.act(Rw(R_BETA), Rw(R_BETA), AF.Ln, [rows, sm], [rows], bias=sm[0:4, 4:5])
            S.act(Rw(R_BETA), Rw(R_BETA), AF.Exp, [rows], [rows], scale=-1.0)
            yield "any"
            S.scan(Rw(R_GC), self.resetm[0:4, :], Rw(R_G), 0.0, [self.resetm, rows], [rows])
            S.ts("dve", Rw(R_NGC), Rw(R_GC), -1.0, None, ALU.mult, None, [rows], [rows])
            S.act(Rw(R_EGC), Rw(R_GC), AF.Exp, [rows], [rows])
            S.tt("dve", Rw(R_BEGE), Rw(R_BETA), Rw(R_EGC), ALU.mult, [rows], [rows])
            gc3 = Rw(R_GC).rearrange("p (c j) -> p c j", j=64)
            gl = bc(gc3[:, :, 63:64], [4, 4, 64])
            S.tt("dve", Rw(R_TMP).rearrange("p (c j) -> p c j", j=64), gl, gc3, ALU.subtract, [rows], [rows])
            S.act(Rw(R_EKD), Rw(R_TMP), AF.Exp, [rows], [rows])
            S.act(Rw(R_AL).rearrange("p (c j) -> p c j", j=64), gl, AF.Exp, [rows], [rows])
            yield "pe"
            tb4 = tb.t.rearrange("p (q c) -> p q c", q=4)
            for i in range(2):
                for qi, ri in enumerate((R_NGC, R_BETA, R_BEGE, R_EKD)):
                    S.tr(tb4[:, qi, :], rows[:, ri, i * 128:(i + 1) * 128], identf[:], [rows, identf], [tb])
                S.copy("dve", cols[:, i, :, :], tb4[:, :, 0:4], [tb], [cols])
                yield "pe"
            pnf = self.psum[:, 3 * 512:4 * 512]
            for h in range(4):
                S.mm(pnf, sel[:, h, :], rows[:, R_GC:R_GC + 2, :], True, True, [sel, rows], [pn_])
                S.copy("act", GCb[:, h, :, :], pnf.rearrange("p (a b) -> p a b", a=2), [pn_], [GCb])
                S.mm(tb[:], sel[:, h, :], rows[:, R_EGC:R_EGC + 2, :], True, True, [sel, rows], [tb])
                if own:
                    S.copy("act", EGb[:, h, :], tb[:, 0:NB], [tb], [EGb])
                S.copy("dve", ALc[:, h, :], tb[:, NB:2 * NB:64], [tb], [ALc])
                yield "pe"
            groups = [("k", h) for h in range(4)] + [("v", h) for h in range(4)]
            if own:
                groups += [("q", h) for h in range(4)]
            if b == 7:
                groups += [("qh", h) for h in range(4)]
            if own:
                groups += [("z", h) for h in range(4)]
            def do_proj(kind, h):
                base = {"q": OFF_Q, "k": OFF_K, "v": OFF_V, "z": OFF_Z, "qh": OFF_Q}[kind]
                p_ = proj(base + h * 128)
                if kind == "qh":
                    S.copy("act", raw[:, h, 3:3 + NB], p_[:], [p_], [raw])
                    S.copy("pool", raw[:, h, 0:3], raw[:, h, NB:NB + 3], [raw], [raw])
                    return None
                if kind == "z":
                    S.act(zs[:, h, :], p_[:], AF.Silu, [p_], [zs])
                    return None
                fb = {"q": 0, "k": 4, "v": 8}[kind] + h
                S.copy("act", raw[:, fb, 3:3 + NB], p_[:], [p_], [raw])
                return (kind, h, fb)

            def do_conv(kind, h, fb):
                a_ = acc[fb % 2]
                S.ts("dve", a_[:], raw[:, fb, 0:NB], cw[:, fb, 0:1], None, ALU.mult, None, [raw, cw], [a_])
                for j in range(1, 4):
                    S.stt("dve", a_[:], raw[:, fb, j:j + NB], cw[:, fb, j:j + 1], a_[:], ALU.mult, ALU.add, [raw, cw, a_], [a_])
                S.copy("pool", raw[:, fb, 0:3], raw[:, fb, NB:NB + 3], [raw], [raw])
                dst = {"q": qT, "k": kT, "v": vT}[kind]
                S.act(dst[:, h, :], a_[:], AF.Silu, [a_], [dst])

            convs = []
            pend = []
            for gi, (kind, h) in enumerate(groups):
                r_ = do_proj(kind, h)
                if r_ is not None:
                    convs.append(r_)
                    pend.append(r_)
                if len(pend) >= 2:
                    yield "ve"
                    do_conv(*pend.pop(0))
                yield "pe" if gi + 1 < len(groups) else "ve"
            while pend:
                do_conv(*pend.pop(0))
                yield "ve"
            for (kind, h, fb) in convs:
                if kind == "v":
                    continue
                dst = {"q": qT, "k": kT}[kind]
                sq_ = sqb[fb % 2]
                S.act(sq_[:], dst[:, h, :], AF.Square, [dst], [sq_])
                S.mm(pn_[:], self.onesb[:], sq_[:], True, True, [self.onesb, sq_], [pn_])
                r_ = rn[fb % 2]
                S.act(r_[:], pn_[:], AF.Ln, [pn_], [r_], bias=self.epsc[:, 0:1])
                if kind == "q":
                    S.act(r_[:], r_[:], AF.Exp, [r_, sm], [r_], scale=-0.5, bias=sm[:, 3:4])
                else:
                    S.act(r_[:], r_[:], AF.Exp, [r_], [r_], scale=-0.5)
                S.tt("dve", dst[:, h, :], dst[:, h, :], r_[:], ALU.mult, [dst, r_], [dst])
                yield "ve"
            if own:
                for h in range(4):
                    S.tt("pool", qgT[h][:], qT[:, h, :], EGb[:, h, :], ALU.mult, [qT, EGb], [qgT[h]])
                yield "ve"

        gen = None
        nkind = "any"

        def pump(kind=None):
            nonlocal gen, nkind
            if gen is None:
                return
            if kind is not None and nkind != "any" and nkind != kind:
                return
            try:
                nkind = next(gen)
            except StopIteration:
                gen = None

        def back(b, F):
            nonlocal npair
            own = b >= 8
            qT, kT, vT, zs, qgT, GCb, BEb, cols, ALc = (F[n_] for n_ in ("qT", "kT", "vT", "zs", "qgT", "GCb", "BEb", "cols", "ALc"))
            H4 = range(4)
            for pt in range(2):
                sl = slice(pt * 128, (pt + 1) * 128)
                ix = lambda h: pt * 4 + h
                for h in H4:
                    S.mm(pk[h][:], kT[:, h, sl], kT[:, h, sl], True, True, [kT], [pk[h]])
                for h in H4:
                    S.stt("dve", d1[h][:], GCv[id(GCb)][:, h, sl], cols[:, pt, 0, h:h + 1], self.negmask[:], ALU.add, ALU.add,
                          [GCb, cols, self.negmask], [d1[h]])
                    S.act(dec[h][:], d1[h][:], AF.Exp, [d1[h]], [dec[h]])
                    S.tt("pool", decs[h][:], dec[h][:], self.strictm[:], ALU.mult, [dec[h], self.strictm], [decs[h]])
                pump("pe")
                for h in H4:
                    S.tt("dve", tbf[h][:], pk[h][:], BEv[id(BEb)][:, h, sl], ALU.mult, [pk[h], BEb], [tbf[h]])
                    S.stt("dve", Pm[h][:], tbf[h][:], -1.0, decs[h][:], ALU.mult, ALU.mult, [tbf[h], decs[h]], [Pm[h]])
                if own:
                    for h in H4:
                        S.mm(pq[h][:], kT[:, h, sl], qT[:, h, sl], True, True, [kT, qT], [pq[h]])
                    for h in H4:
                        S.tt("dve", attnT[ix(h)][:], pq[h][:], dec[h][:], ALU.mult, [pq[h], dec[h]], [attnT[ix(h)]])
                pump("pe")
                for h in H4:
                    S.tr(pu[h][:], Pm[h][:], identf[:], [Pm[h], identf], [pu[h]])
                for h in H4:
                    S.copy("act", Um[h][:], pu[h][:], [pu[h]], [Um[h]])
                    S.tt("pool", Wm[h][:], Pm[h][:], identf[:], ALU.add, [Pm[h], identf], [Wm[h]])
                pump("pe")
                for k in range(6):
                    for h in H4:
                        if k < 5:
                            S.mm(pu[h][:], Pm[h][:], Um[h][:], True, True, [Pm[h], Um[h]], [pu[h]])
                        if k < 4:
                            S.mm(pp[h][:], Um[h][:], Pm[h][:], True, True, [Pm[h], Um[h]], [pp[h]])
                    for h in H4:
                        if k >= 1:
                            S.mm(pw[h][:], Um[h][:], Wm[h][:], True, True, [Um[h], Wm[h]], [pw[h]])
                    pump("ve")
                    for h in H4:
                        if k < 5:
                            S.copy("act", Um[h][:], pu[h][:], [pu[h]], [Um[h]])
                        if k < 4:
                            S.copy("dve", Pm[h][:], pp[h][:], [pp[h]], [Pm[h]])
                    for h in H4:
                        if k >= 1:
                            dstW = Wbf[ix(h)] if k == 5 else Wm[h]
                            S.tt("dve", dstW[:], pw[h][:], Wm[h][:], ALU.add, [pw[h], Wm[h]], [dstW])
                    pump("pe")
                for h in H4:
                    pkb = pk[h].t.bitcast(BF16)[:, 0:128]
                    pqb = pq[h].t.bitcast(BF16)[:, 0:128]
                    S.tr(pkb, kT[:, h, sl], identb[:], [kT, identb], [pk[h]])
                    S.tr(pqb, vT[:, h, sl], identb[:], [vT, identb], [pq[h]])
                pump("ve")
                for h in H4:
                    pkb = pk[h].t.bitcast(BF16)[:, 0:128]
                    pqb = pq[h].t.bitcast(BF16)[:, 0:128]
                    S.ts("dve", RHSk[ix(h)][:], pkb, cols[:, pt, 2, h:h + 1], None, ALU.mult, None, [pk[h], cols], [RHSk[ix(h)]])
                    S.act(kdec[ix(h)][:], pkb, AF.Copy, [pk[h], cols], [kdec[ix(h)]], scale=cols[:, pt, 3, h:h + 1])
                    S.ts("dve", RHSv[ix(h)][:], pqb, cols[:, pt, 1, h:h + 1], None, ALU.mult, None, [pq[h], cols], [RHSv[ix(h)]])
                pump("pe")
                for h in H4:
                    S.mm(pu[h][:], Wbf[ix(h)][:], RHSv[ix(h)][:], True, True, [Wbf[ix(h)], RHSv[ix(h)]], [pu[h]])
                    S.mm(pp[h][:], RHSk[ix(h)][:], Wbf[ix(h)][:], True, True, [Wbf[ix(h)], RHSk[ix(h)]], [pp[h]])
                pump("ve")
                for h in H4:
                    S.copy("act", uu[ix(h)][:], pu[h][:], [pu[h]], [uu[ix(h)]])
                    S.copy("dve", wT[ix(h)][:], pp[h][:], [pp[h]], [wT[ix(h)]])
                pump("pe")
                r0 = (2 * npair) % 3
                for ci in range(2):
                    rws = slice(ci * 64, (ci + 1) * 64)
                    rc, rnx = (r0 + ci) % 3, (r0 + ci + 1) % 3
                    vn = vnA if ci == 0 else vnB
                    for h in H4:
                        S.mm(pw[h][:], wT[ix(h)][:], Sbf[h][rc][:], True, True, [wT[ix(h)], Sbf[h][rc]], [pw[h]])
                    pump("ve")
                    for h in H4:
                        S.tt("dve", vn[h][rws, :], uu[ix(h)][rws, :], pw[h][rws, :], ALU.subtract, [uu[ix(h)], pw[h]], [vn[h]])
                    pump("pe")
                    for h in H4:
                        S.mm(pq[h][:], kdec[ix(h)][:], vn[h][:], True, True, [kdec[ix(h)], vn[h]], [pq[h]])
                    pump("ve")
                    for h in H4:
                        al_ = ALc[:, h, pt * 2 + ci:pt * 2 + ci + 1]
                        S.stt("dve", Sbf[h][rnx][:], S32[h][:], al_, pq[h][:], ALU.mult, ALU.add, [S32[h], ALc, pq[h]], [Sbf[h][rnx]])
                        S.stt("dve", S32[h][:], S32[h][:], al_, pq[h][:], ALU.mult, ALU.add, [S32[h], ALc, pq[h]], [S32[h]])
                    pump("pe")
                if own:
                    t0 = pt * 128
                    for h in H4:
                        S.mm(pk[h][:, 0:64], Sbf[h][r0][:], qgT[h][:, t0:t0 + 64], True, False, [Sbf[h][r0], qgT[h]], [pk[h]])
                        S.mm(pk[h][:, 64:128], Sbf[h][(r0 + 1) % 3][:], qgT[h][:, t0 + 64:t0 + 128], False, False,
                             [Sbf[h][(r0 + 1) % 3], qgT[h]], [pk[h]])
                        S.mm(pk[h][:], vnA[h][:], attnT[ix(h)][:], False, False, [vnA[h], attnT[ix(h)]], [pk[h]])
                        S.mm(pk[h][:], vnB[h][:], attnT[ix(h)][:], False, True, [vnB[h], attnT[ix(h)]], [pk[h]])
                    for h in H4:
                        S.copy("act", oraw[h][:, sl], pk[h][:], [pk[h]], [oraw[h]])
                    pump("pe")
                npair += 1
            if own:
                blk = (b - 8) // 2
                c0 = ((b - 8) % 2) * NB
                ob = self.OY[blk]
                for h in range(4):
                    sq_ = sqb[h % 2]
                    S.act(sq_[:], oraw[h][:], AF.Square, [oraw[h]], [sq_])
                    S.mm(pn_[:], self.onesb[:], sq_[:], True, True, [self.onesb, sq_], [pn_])
                    r_ = rn[h % 2]
                    S.act(r_[:], pn_[:], AF.Ln, [pn_], [r_], scale=1.0 / 128, bias=self.epsc[:, 0:1])
                    S.act(r_[:], r_[:], AF.Exp, [r_], [r_], scale=-0.5)
                    q_ = acc[h % 2]
                    S.tt("dve", q_[:], oraw[h][:], r_[:], ALU.mult, [oraw[h], r_], [q_])
                    S.stt("dve", ob[:, h, c0:c0 + NB], q_[:], sm[:, 2:3], zs[:, h, :], ALU.mult, ALU.mult, [q_, sm, zs], [ob])

        import os
        nblk = int(os.environ.get("DN_NBLK", "16"))
        gen = front(0, FS[0])
        while gen is not None:
            pump()
        for b in range(nblk):
            gen = front(b + 1, FS[(b + 1) % 2]) if b + 1 < nblk else None
            nkind = "any"
            back(b, FS[b % 2])
            while gen is not None:
                pump()

    def phase_x0(self):
        S, A, I = self.S, self.A, self.I
        hall = A.alloc("h", [128, 16, 1024])
        self.H = [Buf("h%d" % i, hall.t[:, i, :]) for i in range(16)]
        self.mark_h = A.off
        wout = A.alloc("wout", [128, 8, 1024], BF16)
        for k in range(8):
            S.dma("pool", wout[:, k, :], I["w_out"][0, k * 128:(k + 1) * 128, :], writes=[wout])
        self.wq = A.alloc("wq", [128, 8, 1024], BF16)
        self.wo = A.alloc("wo", [128, 8, 1024], BF16)
        self.off_after_w = A.off
        for k in range(8):
            S.dma("pool", self.wq[:, k, :], I["w_xq"][0, k * 128:(k + 1) * 128, :], writes=[self.wq])
            S.dma("pool", self.wo[:, k, :], I["w_xo"][0, k * 128:(k + 1) * 128, :], writes=[self.wo])
        pb = [self.bank(i, "x0b%d" % i) for i in range(4)]
        for i in range(16):
            hb = self.H[i]
            S.dma("sp", hb[:], I["xe"][T_OWN + i * 128:T_OWN + (i + 1) * 128, :], writes=[hb])
            ob = self.OY[i // 4]
            c0 = (i % 4) * 128
            for hf in range(2):
                p_ = pb[(i * 2 + hf) % 4]
                for k in range(8):
                    S.mm(p_[:], ob[:, k, c0:c0 + 128], wout[:, k, hf * 512:(hf + 1) * 512], k == 0, k == 7, [ob, wout], [p_])
                S.tt("dve", hb[:, hf * 512:(hf + 1) * 512], hb[:, hf * 512:(hf + 1) * 512], p_[:], ALU.add, [hb, p_], [hb])
        A.off = self.mark_h

    def phase_x1(self):
        S, A, I = self.S, self.A, self.I
        A0 = Arena(self.A.t, 8192)
        wK = A0.alloc("wK", [128, 8, 1024], BF16)
        wV = A0.alloc("wV", [128, 8, 1024], BF16)
        gain = A.alloc("gx", [128, 1024])
        xn = A.alloc("xn", [128, 1024], BF16)
        junk = xn
        xs = A.alloc("xs", [128, 1024])
        mnT = A.alloc("mnT", [128, 8, NMEM], BF16)
        ss = A.alloc("ss", [128, 1])
        rs = A.alloc("rs", [128, 1])
        assert A.off <= self.mark_h + 4096
        A.off = self.off_after_w
        KT = A.alloc("KT", [128, 8, NMEM], BF16)
        Vm = A.alloc("Vm", [128, 2, 1024], BF16)
        xnT = A.alloc("xnT", [128, 8, 512], BF16)
        qxT = A.alloc("qxT", [128, 8, 512], BF16)
        tb = self.bank(0, "x1tb")
        pb = [self.bank(i, "x1b%d" % i) for i in range(1, 8)]
        npb = 0

        def nb():
            nonlocal npb
            npb += 1
            return pb[npb % 7]
        S.dma("sp", gain[:], I["norm_mem_g"][0:1, :].partition_broadcast(128), writes=[gain])
        for k in range(8):
            S.dma("pool", wK[:, k, :], I["w_xk"][0, k * 128:(k + 1) * 128, :], writes=[wK])
        for k in range(8):
            S.dma("pool", wV[:, k, :], I["w_xv"][0, k * 128:(k + 1) * 128, :], writes=[wV])
        for i in range(2):
            S.dma("sp", xs[:], I["mem"][i * 128:(i + 1) * 128, :], writes=[xs])
            self.norm_to_T(xs[:], xs, gain, mnT, i * 128, tb, (junk, ss, rs, xn))
        for cb in range(8):
            p_ = nb()
            for k in range(8):
                S.mm(p_[:, 0:NMEM], wK[:, k, cb * 128:(cb + 1) * 128], mnT[:, k, :], k == 0, k == 7, [wK, mnT], [p_])
            S.copy("act", KT[:, cb, :], p_[:, 0:NMEM], [p_], [KT])
        for mt in range(2):
            for hf in range(2):
                p_ = nb()
                for k in range(8):
                    S.mm(p_[:], mnT[:, k, mt * 128:(mt + 1) * 128], wV[:, k, hf * 512:(hf + 1) * 512], k == 0, k == 7, [wV, mnT], [p_])
                S.copy("act", Vm[:, mt, hf * 512:(hf + 1) * 512], p_[:], [p_], [Vm])
        S.barrier()
        A0 = Arena(self.A.t, 8192)
        ET = [A0.alloc("ET%d" % i, [128, 2, 512], BF16) for i in range(2)]
        rinv = [A0.alloc("rinv%d" % i, [128, 512]) for i in range(2)]
        oxT = A0.alloc("oxT", [128, 8, 512], BF16)
        wA, wo = self.wq, self.wo
        S.dma("sp", gain[:], I["norm_x_g"][0:1, :].partition_broadcast(128), writes=[gain])
        xnT2 = [xnT, A.alloc("xnTb", [128, 8, 512], BF16)]
        qxT2 = [qxT, A.alloc("qxTb", [128, 8, 512], BF16)]

        def xfront(blk):
            xnT_, qxT_ = xnT2[blk % 2], qxT2[blk % 2]
            for i in range(4):
                hb = self.H[blk * 4 + i]
                self.norm_to_T(hb[:], hb, gain, xnT_, i * 128, tb, (junk, ss, rs, xn))
                yield
            for cb in range(8):
                p_ = nb()
                for k in range(8):
                    S.mm(p_[:], wA[:, k, cb * 128:(cb + 1) * 128], xnT_[:, k, :], k == 0, k == 7, [wA, xnT_], [p_])
                S.copy("act", qxT_[:, cb, :], p_[:], [p_], [qxT_])
                yield

        gen = None

        def pump():
            nonlocal gen
            if gen is None:
                return
            try:
                next(gen)
            except StopIteration:
                gen = None

        gen = xfront(0)
        while gen is not None:
            pump()
        for blk in range(4):
            qxT_ = qxT2[blk % 2]
            gen = xfront(blk + 1) if blk + 1 < 4 else None
            for hd in range(4):
                E_ = ET[hd % 2]
                for mt in range(2):
                    p_ = nb()
                    for hf in range(2):
                        S.mm(p_[:], KT[:, hd * 2 + hf, mt * 128:(mt + 1) * 128], qxT_[:, hd * 2 + hf, :], hf == 0, hf == 1, [KT, qxT_], [p_])
                    S.act(E_[:, mt, :], p_[:], AF.Exp, [p_], [E_], scale=1.0 / 16.0)
                pump()
                p_ = nb()
                for mt in range(2):
                    S.mm(p_[:], self.onesb[:], E_[:, mt, :], mt == 0, mt == 1, [self.onesb, E_], [p_])
                r_ = rinv[hd % 2]
                S.act(r_[:], p_[:], AF.Ln, [p_], [r_])
                S.act(r_[:], r_[:], AF.Exp, [r_], [r_], scale=-1.0)
                pump()
                for hf in range(2):
                    p_ = nb()
                    for mt in range(2):
                        S.mm(p_[:], Vm[:, mt, hd * 256 + hf * 128:hd * 256 + (hf + 1) * 128], E_[:, mt, :], mt == 0, mt == 1, [Vm, E_], [p_])
                    S.tt("dve", oxT[:, hd * 2 + hf, :], p_[:], r_[:], ALU.mult, [p_, r_], [oxT])
            for i in range(4):
                hb = self.H[blk * 4 + i]
                for hf in range(2):
                    p_ = nb()
                    for c in range(8):
                        S.mm(p_[:], oxT[:, c, i * 128:(i + 1) * 128], wo[:, c, hf * 512:(hf + 1) * 512], c == 0, c == 7, [oxT, wo], [p_])
                    S.tt("dve", hb[:, hf * 512:(hf + 1) * 512], hb[:, hf * 512:(hf + 1) * 512], p_[:], ALU.add, [hb, p_], [hb])
                pump()
            while gen is not None:
                pump()
        A.off = self.mark_h

    def phase_f(self):
        S, A, I = self.S, self.A, self.I
        A0 = Arena(self.A.t, 8192)
        gain = A0.alloc("gf", [128, 1024])
        junk = A0.alloc("junk", [128, 1024], BF16)
        xn = A0.alloc("xn", [128, 1024], BF16)
        sgl = [A0.alloc("sgl%d" % i, [128, 512]) for i in range(2)]
        hid = [A0.alloc("hid%d" % i, [128, 4, 512], BF16) for i in range(2)]
        ost = [A0.alloc("ost%d" % i, [128, 1024]) for i in range(2)]
        ss = A.alloc("ss", [128, 1])
        rs = A.alloc("rs", [128, 1])
        xnT = A.alloc("xnTall", [128, 8, T_OWN], BF16)
        wg = [A.alloc("wg%d" % i, [128, 8, 512], BF16) for i in range(2)]
        wuU = [A.alloc("wup%d" % i, [128, 8, 512], BF16) for i in range(2)]
        wd = [A.alloc("wd%d" % i, [128, 4, 1024], BF16) for i in range(2)]
        tb = self.bank(0, "ftb")
        pb = [self.bank(i, "fb%d" % i) for i in range(1, 8)]
        npb = 0

        def nb():
            nonlocal npb
            npb += 1
            return pb[npb % 7]
        S.dma("sp", gain[:], I["norm_ffn_g"][0:1, :].partition_broadcast(128), writes=[gain])
        gfin = A.alloc("gfin", [128, 1024])
        S.dma("sp", gfin[:], I["norm_final_g"].rearrange("(o d) -> o d", o=1).partition_broadcast(128), writes=[gfin])
        slices = [(c0, min(4, 22 - c0)) for c0 in range(0, 22, 4)]

        def load_slice(si):
            c0, nc_ = slices[si]
            w = nc_ * 128
            for k in range(8):
                S.dma("pool", wg[si % 2][:, k, 0:w], I["w_gate"][0, k * 128:(k + 1) * 128, c0 * 128:c0 * 128 + w], writes=[wg[si % 2]])
                S.dma("pool", wuU[si % 2][:, k, 0:w], I["w_up"][0, k * 128:(k + 1) * 128, c0 * 128:c0 * 128 + w], writes=[wuU[si % 2]])
            for c in range(nc_):
                S.dma("pool", wd[si % 2][:, c, :], I["w_down"][0, (c0 + c) * 128:(c0 + c + 1) * 128, :], writes=[wd[si % 2]])
        load_slice(0)
        for i in range(16):
            self.norm_to_T(self.H[i][:], self.H[i], gain, xnT, i * 128, tb, (junk, ss, rs, xn))
        for si in range(len(slices)):
            if si + 1 < len(slices):
                load_slice(si + 1)
            c0, nc_ = slices[si]
            g_, u_, d_ = wg[si % 2], wuU[si % 2], wd[si % 2]
            for blk in range(4):
                tsl = slice(blk * 512, (blk + 1) * 512)
                hd_ = hid[blk % 2]
                for c in range(nc_):
                    pg = nb()
                    for k in range(8):
                        S.mm(pg[:], g_[:, k, c * 128:(c + 1) * 128], xnT[:, k, tsl], k == 0, k == 7, [g_, xnT], [pg])
                    pu_ = nb()
                    for k in range(8):
                        S.mm(pu_[:], u_[:, k, c * 128:(c + 1) * 128], xnT[:, k, tsl], k == 0, k == 7, [u_, xnT], [pu_])
                    s_ = sgl[c % 2]
                    S.act(s_[:], pg[:], AF.Silu, [pg], [s_])
                    S.tt("dve", hd_[:, c, :], s_[:], pu_[:], ALU.mult, [s_, pu_], [hd_])
                for i in range(4):
                    hb = self.H[blk * 4 + i]
                    for hf in range(2):
                        p_ = nb()
                        for c in range(nc_):
                            S.mm(p_[:], hd_[:, c, i * 128:(i + 1) * 128], d_[:, c, hf * 512:(hf + 1) * 512], c == 0, c == nc_ - 1, [hd_, d_], [p_])
                        S.tt("dve", hb[:, hf * 512:(hf + 1) * 512], hb[:, hf * 512:(hf + 1) * 512], p_[:], ALU.add, [hb, p_], [hb])
                if si == len(slices) - 1:
                    for i in range(4):
                        t_ = blk * 4 + i
                        hb = self.H[t_]
                        o_ = ost[t_ % 2]
                        self.rms_stats(hb[:], hb, junk, ss, rs)
                        S.stt("dve", o_[:], hb[:], rs[:, 0:1], gfin[:], ALU.mult, ALU.mult, [hb, rs, gfin], [o_])
                        S.dma("sp", self.y[t_ * 128:(t_ + 1) * 128, :], o_[:], reads=[o_])
        self.final.extend(ost)


_CACHE = {}


def _core_inputs(inputs, c):
    b, half = c // 2, c % 2
    x = inputs["x"]
    xe = np.zeros((T_EXT, D), np.float32)
    if half == 1:
        xe[:T_OWN] = x[b, :T_OWN]
        xe[T_OWN:] = x[b, T_OWN:]
    else:
        xe[T_OWN:] = x[b, :T_OWN]
    m = {"xe": xe, "mem": np.ascontiguousarray(inputs["mem"][b], dtype=np.float32)}
    for k, v in inputs.items():
        if k in ("x", "mem"):
            continue
        m[k] = np.ascontiguousarray(v, dtype=np.float32)
    return m


def kernel(**inputs):
    inputs = {k: np.asarray(v) for k, v in inputs.items()}
    if "prog" not in _CACHE:
        _CACHE["prog"] = Prog()
    prog = _CACHE["prog"]
    in_maps = [_core_inputs(inputs, c) for c in range(8)]
    res = run_bass_kernel_spmd(prog.nc, in_maps, core_ids=list(range(8)))
    out = np.zeros((4, 4096, D), np.float32)
    for c in range(8):
        b, half = c // 2, c % 2
        out[b, half * T_OWN:(half + 1) * T_OWN] = res.results[c]["y"]
    return out
```

```python
import contextlib
import math
import numpy as np
import concourse.bass as bass
import concourse.mybir as mybir
from concourse.bass_utils import run_bass_kernel_spmd

F32 = mybir.dt.float32
BF16 = mybir.dt.bfloat16
AF = mybir.ActivationFunctionType
ALU = mybir.AluOpType

D = 1024
T_OWN = 2048
T_EXT = 4096
NMEM = 256
DIN = 2568
OFF_Q, OFF_K, OFF_V, OFF_Z, OFF_A, OFF_B, OFF_U = 0, 512, 1024, 1536, 2048, 2052, 2056
DFF = 2816
EPS = 1e-6
PI = math.pi
NEG = -30000.0


class Buf:
    __slots__ = ("name", "w", "r", "dsem", "dcnt", "t", "lock")

    def __init__(self, name, t=None, lock=None):
        self.name = name
        self.lock = lock
        self.w = None
        self.r = {}
        self.dsem = None
        self.dcnt = 0
        self.t = t

    def __getitem__(self, k):
        return self.t[k]


class Sched:
    ENG = ("pe", "act", "dve", "pool", "sp")

    def __init__(self, nc, es):
        self.nc = nc
        self.es = es
        self.q = {e: [] for e in self.ENG}
        self.cnt = {e: 0 for e in self.ENG}
        self.sems = {}
        self.waited = {}
        self.nbuf = 0
        self.dbufs = []
        self.EP = 2000

    def _etok(self, eng, cnt):
        key = "%s#%d" % (eng, (cnt - 1) // self.EP)
        if key not in self.sems:
            self.sems[key] = self.es.enter_context(self.nc.semaphore("s_%s_%d" % (eng, (cnt - 1) // self.EP)))
        return (key, (cnt - 1) % self.EP + 1)

    def _dsem(self, b):
        if b.dsem is None:
            key = "d%d" % self.nbuf
            self.nbuf += 1
            self.sems[key] = self.es.enter_context(self.nc.semaphore(key))
            b.dsem = key
            self.dbufs.append(b)
        return b.dsem

    def _waits(self, eng, toks):
        out = []
        for tk in toks:
            if tk is None:
                continue
            key, val = tk
            if eng == "pe" and key.startswith("pe#"):
                continue
            if self.waited.get((eng, key), 0) >= val:
                continue
            self.waited[(eng, key)] = val
            out.append((key, val))
        return out

    def op(self, eng, fn, reads=(), writes=()):
        locks = {}
        for b in list(reads) + list(writes):
            if b.lock is not None:
                locks[id(b.lock)] = b.lock
        if locks:
            writes = list(writes) + list(locks.values())
        toks = []
        for b in reads:
            toks.append(b.w)
        pre = eng + "#"
        for b in writes:
            toks.append(b.w)
            toks.extend(b.r.items())
        waits = self._waits(eng, toks)
        self.cnt[eng] += 1
        tok = self._etok(eng, self.cnt[eng])
        self.q[eng].append((waits, fn, (tok[0], 1)))
        for b in reads:
            b.r[tok[0]] = tok[1]
        for b in writes:
            b.w = tok
            b.r = {}
        return tok

    def dma(self, eng, out, in_, reads=(), writes=(), slow=False):
        toks = []
        for b in reads:
            toks.append(b.w)
        for b in writes:
            if b.w is not None and b.w[0] != b.dsem:
                toks.append(b.w)
            toks.extend(b.r.items())
        waits = self._waits(eng, toks)
        tb = (list(writes) + list(reads))[0]
        key = self._dsem(tb)
        tb.dcnt += 1
        tok = (key, 16 * tb.dcnt)
        if slow:
            self.q[eng].append((waits, lambda e: e.dma_start(out=out, in_=in_, allow_slow_non_contiguous=True), (key, 16)))
        else:
            self.q[eng].append((waits, lambda e: e.dma_start(out=out, in_=in_), (key, 16)))
        for b in reads:
            b.r[tok[0]] = tok[1]
        for b in writes:
            b.w = tok
            b.r = {}
        return tok

    def barrier(self):
        toks = [self._etok(e, self.cnt[e]) for e in self.ENG if self.cnt[e] > 0]
        for b in self.dbufs:
            toks.append((b.dsem, 16 * b.dcnt))
        for e in self.ENG:
            w = self._waits(e, toks)
            if w:
                self.q[e].append((w, None, None))

    def finish(self, final_bufs):
        toks = []
        for b in final_bufs:
            toks.extend(b.r.items())
            toks.append(b.w)
        fw = self._waits("sp", toks)
        nc = self.nc
        engmap = {"pe": "tensor", "act": "scalar", "dve": "vector", "pool": "gpsimd", "sp": "sync"}
        with nc.Block() as block:
            for e in self.ENG:
                q = self.q[e]
                extra = fw if e == "sp" else []

                def body(engine, q=q, extra=extra):
                    for waits, fn, inc in q:
                        for (wk, wv) in waits:
                            engine.wait_ge(self.sems[wk], wv)
                        if fn is not None:
                            ins = fn(engine)
                            ins.then_inc(self.sems[inc[0]], inc[1])
                    for (wk, wv) in extra:
                        engine.wait_ge(self.sems[wk], wv)

                getattr(block, engmap[e])(body)

    def tt(self, eng, out, in0, in1, op, R, W):
        return self.op(eng, lambda e: e.tensor_tensor(out=out, in0=in0, in1=in1, op=op), R, W)

    def ts(self, eng, out, in0, s1, s2, op0, op1, R, W):
        if op1 is None:
            return self.op(eng, lambda e: e.tensor_scalar(out=out, in0=in0, scalar1=s1, scalar2=None, op0=op0), R, W)
        return self.op(eng, lambda e: e.tensor_scalar(out=out, in0=in0, scalar1=s1, scalar2=s2, op0=op0, op1=op1), R, W)

    def stt(self, eng, out, in0, scalar, in1, op0, op1, R, W):
        return self.op(eng, lambda e: e.scalar_tensor_tensor(out=out, in0=in0, scalar=scalar, in1=in1, op0=op0, op1=op1), R, W)

    def act(self, out, in_, func, R, W, bias=None, scale=None, accum=None):
        kw = {}
        if bias is not None:
            kw["bias"] = bias
        if scale is not None:
            kw["scale"] = scale
        if accum is not None:
            kw["accum_out"] = accum
        return self.op("act", lambda e: e.activation(out=out, in_=in_, func=func, **kw), R, W)

    def copy(self, eng, out, in_, R, W):
        if eng == "act":
            return self.act(out, in_, AF.Copy, R, W)
        return self.op(eng, lambda e: e.tensor_copy(out=out, in_=in_), R, W)

    def memset(self, eng, ap, val, W):
        return self.op(eng, lambda e: e.memset(ap, val), (), W)

    def mm(self, out, lhsT, rhs, start, stop, R, W):
        return self.op("pe", lambda e: e.matmul(out, lhsT=lhsT, rhs=rhs, start=start, stop=stop), R, W)

    def tr(self, out, in_, ident, R, W):
        return self.op("pe", lambda e: e.transpose(out=out, in_=in_, identity=ident), R, W)

    def scan(self, out, d0, d1, init, R, W):
        return self.op("dve", lambda e: e.tensor_tensor_scan(out=out, data0=d0, data1=d1, initial=init,
                                                            op0=ALU.mult, op1=ALU.add), R, W)


class Arena:
    def __init__(self, t, nwords):
        self.t = t
        self.n = nwords
        self.off = 0

    def alloc(self, name, shape, dtype=F32):
        nel = 1
        for s in shape[1:]:
            nel *= s
        words = nel if dtype == F32 else (nel + 1) // 2
        words = (words + 1) // 2 * 2
        ap = self.t[:, self.off:self.off + words]
        self.off += words
        assert self.off <= self.n, (name, self.off, self.n)
        if dtype != F32:
            ap = ap.bitcast(dtype)
        ap = ap[:, 0:nel]
        if len(shape) == 3:
            ap = ap.rearrange("p (a b) -> p a b", a=shape[1])
        elif len(shape) == 4:
            ap = ap.rearrange("p (a b c) -> p a b c", a=shape[1], b=shape[2])
        if shape[0] < 128:
            ap = ap[0:shape[0]]
        return Buf(name, ap)


def bc(ap, shape):
    return ap.to_broadcast(list(shape))


class Prog:
    def __init__(self, taps=(), stop_after=None):
        self.taps = set(taps)
        self.stop_after = stop_after
        nc = bass.Bass("TRN2", target_bir_lowering=False)
        self.nc = nc
        dt = nc.dram_tensor
        I = {}

        def din(name, shape):
            I[name] = dt(name, list(shape), F32, kind="ExternalInput").ap()

        din("xe", [T_EXT, D])
        din("mem", [NMEM, D])
        din("norm_mix_g", [1, D])
        din("w_in", [1, D, DIN])
        din("conv_w", [1, 4, 1536])
        din("dn_a_log", [1, 4])
        din("dn_dt_bias", [1, 4])
        din("dn_norm_g", [1, 128])
        din("s5_a_re", [1, 32, 64])
        din("s5_a_im", [1, 32, 64])
        din("s5_b_re", [1, 32, 64, 16])
        din("s5_b_im", [1, 32, 64, 16])
        din("s5_c_re", [1, 32, 16, 64])
        din("s5_c_im", [1, 32, 16, 64])
        din("s5_d", [1, 32, 16])
        din("s5_log_dt", [1, 32])
        din("s5_w_glu", [1, 512, 512])
        din("s5_b_glu", [1, 512])
        din("w_out", [1, D, D])
        din("norm_x_g", [1, D])
        din("norm_mem_g", [1, D])
        din("w_xq", [1, D, D])
        din("w_xk", [1, D, D])
        din("w_xv", [1, D, D])
        din("w_xo", [1, D, D])
        din("norm_ffn_g", [1, D])
        din("w_gate", [1, D, DFF])
        din("w_up", [1, D, DFF])
        din("w_down", [1, DFF, D])
        din("norm_final_g", [D])
        self.I = I
        self.y = dt("y", [T_OWN, D], F32, kind="ExternalOutput").ap()
        self.tapd = {}
        if "oy" in self.taps:
            self.tapd["oy"] = dt("tap_oy", [128, 8, T_OWN], BF16, kind="ExternalOutput").ap()
        if "h" in self.taps:
            self.tapd["h"] = dt("tap_h", [T_OWN, D], F32, kind="ExternalOutput").ap()
        if "dbg" in self.taps:
            self.tapd["dbg"] = dt("tap_dbg", [128, 4096], F32, kind="ExternalOutput").ap()

        with contextlib.ExitStack() as es:
            self.es = es
            S = Sched(nc, es)
            self.S = S
            AW = 46 * 1024
            arena_t = es.enter_context(nc.sbuf_tensor("arena", [128, AW], F32))
            self.A = Arena(arena_t, AW)
            self.psum = es.enter_context(nc.psum_tensor("psum", [128, 4096], F32))
            self.plocks = [Buf("plock%d" % i) for i in range(8)]
            self.final = []
            self.consts()
            self.build()
            S.finish(self.final)

    def cbuf(self, name, shape, dtype=F32):
        t = self.es.enter_context(self.nc.sbuf_tensor(name, list(shape), dtype))
        return Buf(name, t)

    def consts(self):
        S = self.S
        self.identf = self.cbuf("identf", [128, 128])
        self.identb = self.cbuf("identb", [128, 128], BF16)
        self.onesb = self.cbuf("onesb", [128, 128], BF16)
        self.onesf = self.cbuf("onesf", [128, 128])
        self.zerosf = self.cbuf("zerosf", [128, 128])
        self.negmask = self.cbuf("negmask", [128, 128])
        self.strictm = self.cbuf("strictm", [128, 128])
        self.sel = self.cbuf("sel", [128, 4, 128])
        self.resetm = self.cbuf("resetm", [128, 256])
        onesf, zerosf, identf = self.onesf, self.zerosf, self.identf
        S.memset("pool", onesf[:], 1.0, [onesf])
        S.memset("pool", zerosf[:], 0.0, [zerosf])
        S.memset("dve", self.onesb[:], 1.0, [self.onesb])

        def asel(out, in_, cmp, fill, base, cm, pattern, R, W):
            S.op("pool", lambda e: e.affine_select(out=out, in_=in_, pattern=pattern, compare_op=cmp, fill=fill,
                                                   base=base, channel_multiplier=cm), R, W)

        asel(identf[:], onesf[:], ALU.is_equal, 0.0, 0, -1, [[1, 128]], [onesf], [identf])
        S.copy("dve", self.identb[:], identf[:], [identf], [self.identb])
        asel(self.negmask[:], zerosf[:], ALU.is_ge, NEG, 0, -1, [[1, 128]], [zerosf], [self.negmask])
        S.memset("pool", self.negmask[0:64, 64:128], NEG, [self.negmask])
        asel(self.strictm[:], onesf[:], ALU.is_ge, 0.0, -1, -1, [[1, 128]], [onesf], [self.strictm])
        S.memset("pool", self.strictm[0:64, 64:128], 0.0, [self.strictm])
        for h in range(4):
            asel(self.sel[:, h, :], onesf[:], ALU.is_equal, 0.0, -h, 1, [[0, 128]], [onesf], [self.sel])
        S.memset("dve", self.resetm[:], 1.0, [self.resetm])
        for c in range(4):
            S.memset("dve", self.resetm[:, c * 64:c * 64 + 1], 0.0, [self.resetm])

    def bank(self, i, name=None):
        return Buf(name or ("bank%d" % i), self.psum[:, i * 512:(i + 1) * 512], lock=self.plocks[i])

    def load_w_bf16(self, dst, src, nk, chunk_cols=None):
        for k in range(nk):
            self.S.dma("pool", dst[:, k, :], src[k * 128:(k + 1) * 128, :], writes=[dst])

    def rms_stats(self, xap, xbuf, junk, ss, rs, nfeat=1024):
        S = self.S
        S.act(junk[:], xap, AF.Square, [xbuf], [junk, ss], accum=ss[:])
        S.act(rs[:], ss[:], AF.Ln, [ss], [rs], scale=1.0 / nfeat, bias=self.epsc[:, 0:1])
        S.act(rs[:], rs[:], AF.Exp, [rs], [rs], scale=-0.5)

    def norm_to_T(self, xap, xbuf, gain, xnT, col0, tb, tmp):
        S = self.S
        junk, ss, rs, xn = tmp
        self.rms_stats(xap, xbuf, junk, ss, rs)
        S.stt("dve", xn[:], xap, rs[:, 0:1], gain[:], ALU.mult, ALU.mult, [xbuf, rs, gain], [xn])
        tv = tb.t.bitcast(BF16).rearrange("p (a b) -> p a b", a=8)
        for k in range(8):
            S.tr(tv[:, k, :], xn[:, k * 128:(k + 1) * 128], self.identb[:], [xn, self.identb], [tb])
        S.copy("act", xnT[:, :, col0:col0 + 128], tv, [tb], [xnT])

    def build(self):
        S, A, I = self.S, self.A, self.I
        self.epsc = self.cbuf("epsc", [128, 1])
        S.memset("dve", self.epsc[:], EPS, [self.epsc])
        oy_all = A.alloc("oy", [128, 8, T_OWN], BF16)
        self.OY = [Buf("oy%d" % b, oy_all.t[:, :, b * 512:(b + 1) * 512]) for b in range(4)]
        self.oy_all = oy_all
        mark = A.off
        import os
        if not os.environ.get("SKIP_S5"):
            self.sweep_s5()
            S.barrier()
        A.off = mark
        if self.stop_after == "s5":
            return self.finish_taps()
        self.sweep_dn()
        S.barrier()
        A.off = mark
        if self.stop_after == "dn":
            return self.finish_taps()
        self.phase_x0()
        S.barrier()
        self.phase_x1()
        S.barrier()
        if self.stop_after == "x":
            return self.finish_taps()
        self.phase_f()
        self.finish_taps()

    def finish_taps(self):
        S = self.S
        if "oy" in self.tapd:
            for b in range(4):
                S.dma("sp", self.tapd["oy"][:, :, b * 512:(b + 1) * 512], self.OY[b][:], reads=[self.OY[b]])
                self.final.append(self.OY[b])
        if "h" in self.tapd and hasattr(self, "H"):
            for i in range(16):
                S.dma("sp", self.tapd["h"][i * 128:(i + 1) * 128, :], self.H[i][:], reads=[self.H[i]])
                self.final.append(self.H[i])
        if "dbg" in self.tapd and hasattr(self, "dbg"):
            S.dma("sp", self.tapd["dbg"][:, 0:self.dbg.t.shape[-1]], self.dbg[:], reads=[self.dbg])
            self.final.append(self.dbg)
        if self.stop_after is not None:
            z = self.cbuf("zfill", [128, D])
            S.memset("dve", z[:], 0.0, [z])
            for i in range(16):
                S.dma("sp", self.y[i * 128:(i + 1) * 128, :], z[:], reads=[z])
            self.final.append(z)

    def cmul(self, eng, o_re, o_im, a_re, a_im, b_re, b_im, t1, t2, R, W):
        S = self.S
        S.tt(eng, t1, a_re, b_re, ALU.mult, R, W)
        S.tt(eng, t2, a_im, b_im, ALU.mult, R, W)
        S.tt(eng, o_re, t1, t2, ALU.subtract, R, W)
        S.tt(eng, t1, a_re, b_im, ALU.mult, R, W)
        S.tt(eng, t2, a_im, b_re, ALU.mult, R, W)
        S.tt(eng, o_im, t1, t2, ALU.add, R, W)

    def sweep_s5(self):
        S, A, I = self.S, self.A, self.I
        wu = A.alloc("wu", [128, 8, 512], BF16)
        for k in range(8):
            S.dma("pool", wu[:, k, :], I["w_in"][0, k * 128:(k + 1) * 128, OFF_U:OFF_U + 512], writes=[wu])
        wglu = A.alloc("wglu", [128, 4, 512], BF16)
        for k in range(4):
            S.dma("pool", wglu[:, k, :], I["s5_w_glu"][0, k * 128:(k + 1) * 128, :], writes=[wglu])
        prm = A.alloc("s5prm", [128, 64, 16])
        P = lambda i: prm.t[:, i, :]
        bglu = A.alloc("bglu", [128, 4])
        S.dma("sp", bglu[:], I["s5_b_glu"][0].rearrange("(f p) -> p f", p=128), writes=[bglu], slow=True)
        dcol = A.alloc("dcol", [128, 4])
        S.dma("sp", dcol[:], I["s5_d"][0].rearrange("(j g) c -> (g c) j", g=8), writes=[dcol], slow=True)
        gain = A.alloc("gmix", [128, 1024])
        S.dma("sp", gain[:], I["norm_mix_g"][0:1, :].partition_broadcast(128), writes=[gain])
        ARE, AIM, LDT, DT, MAG, TH, S1, C1, NR, DEN, RDEN, CORE, COIM, T1, T2, T3, ABR, ABI, NTH, HPI = range(20)
        ER = [20 + k for k in range(7)]
        EI = [27 + k for k in range(7)]
        IR = [34, 35]
        II = [36, 37]
        TI = [38, 39, 40, 41]
        S.dma("sp", P(ARE), I["s5_a_re"][0].rearrange("(q t) p -> (t p) q", t=2), writes=[prm], slow=True)
        S.dma("sp", P(AIM), I["s5_a_im"][0].rearrange("(q t) p -> (t p) q", t=2), writes=[prm], slow=True)
        ldv = I["s5_log_dt"][0].rearrange("(q t) -> t q", t=2)
        for t in range(2):
            S.dma("sp", prm.t[t * 64:(t + 1) * 64, LDT, :], ldv[t].partition_broadcast(64), writes=[prm], slow=True)
        TC = A.alloc("TC", [128, 16, 64])
        TSn = A.alloc("TS", [128, 16, 64])
        BL = A.alloc("BL", [128, 16, 2, 128], BF16)
        CL = A.alloc("CL", [128, 16, 2, 128], BF16)
        diagD = A.alloc("diagD", [128, 4, 128], BF16)
        xs = [A.alloc("xs%d" % i, [128, 1024]) for i in range(2)]
        junk = A.alloc("junk", [128, 1024], BF16)
        ss = A.alloc("ss", [128, 1])
        rs = A.alloc("rs", [128, 1])
        xn = A.alloc("xn", [128, 1024], BF16)
        xnT = A.alloc("xnT", [128, 8, 512], BF16)
        uT = [A.alloc("uT%d" % i, [128, 4, 512], BF16) for i in range(2)]
        mark_setup = A.off
        tA = A.alloc("tA", [128, 16, 64])
        tB = A.alloc("tB", [128, 16, 64])
        BPr = A.alloc("BPr", [128, 16, 128])
        BPi = A.alloc("BPi", [128, 16, 128])
        BBr = A.alloc("BBr", [128, 16, 128])
        BBi = A.alloc("BBi", [128, 16, 128])
        tC = A.alloc("tC", [128, 16, 128])
        tD = A.alloc("tD", [128, 16, 128])
        CPr = A.alloc("CPr", [128, 16, 128])
        CPi = A.alloc("CPi", [128, 16, 128])
        for b_ in (BPr, BPi, CPr, CPi):
            S.memset("pool", b_[:], 0.0, [b_])
        for g in range(32):
            q, t = g // 2, g % 2
            c0 = (g % 8) * 16
            S.dma("sp", BPr[t * 64:(t + 1) * 64, q, c0:c0 + 16], I["s5_b_re"][0, g], writes=[BPr])
            S.dma("act", BPi[t * 64:(t + 1) * 64, q, c0:c0 + 16], I["s5_b_im"][0, g], writes=[BPi])
            S.dma("sp", CPr[c0:c0 + 16, q, t * 64:(t + 1) * 64], I["s5_c_re"][0, g], writes=[CPr])
            S.dma("act", CPi[c0:c0 + 16, q, t * 64:(t + 1) * 64], I["s5_c_im"][0, g], writes=[CPi])
        pb = [self.bank(6, "s5b6"), self.bank(7, "s5b7")]
        tb = self.bank(5, "s5tb")
        def front(b):
            for i in range(4):
                x_ = xs[(b * 4 + i) % 2]
                S.dma("sp", x_[:], I["xe"][b * 512 + i * 128:b * 512 + (i + 1) * 128, :], writes=[x_])
                self.norm_to_T(x_[:], x_, gain, xnT, i * 128, tb, (junk, ss, rs, xn))
            uu_ = uT[b % 2]
            for j in range(4):
                pbk = pb[j % 2]
                for k in range(8):
                    S.mm(pbk[:], wu[:, k, j * 128:(j + 1) * 128], xnT[:, k, :], k == 0, k == 7, [wu, xnT], [pbk])
                S.copy("act", uu_[:, j, :], pbk[:], [pbk], [uu_])

        front(0)
        R1 = [prm]
        S.act(P(DT), P(LDT), AF.Exp, R1, R1)
        S.tt("dve", P(T1), P(ARE), P(DT), ALU.mult, R1, R1)
        S.act(P(MAG), P(T1), AF.Exp, R1, R1)
        S.tt("dve", P(TH), P(AIM), P(DT), ALU.mult, R1, R1)
        for _ in range(5):
            S.ts("dve", P(T1), P(TH), PI, -2.0 * PI, ALU.is_gt, ALU.mult, R1, R1)
            S.tt("dve", P(TH), P(TH), P(T1), ALU.add, R1, R1)
        S.act(P(S1), P(TH), AF.Sin, R1, R1)
        S.ts("dve", P(NTH), P(TH), -1.0, None, ALU.mult, None, R1, R1)
        S.tt("dve", P(NTH), P(NTH), P(TH), ALU.max, R1, R1)
        S.memset("dve", P(HPI), PI / 2, R1)
        S.act(P(C1), P(NTH), AF.Sin, R1, R1, scale=-1.0, bias=prm.t[:, HPI, 0:1])
        S.tt("dve", P(ABR), P(MAG), P(C1), ALU.mult, R1, R1)
        S.tt("dve", P(ABI), P(MAG), P(S1), ALU.mult, R1, R1)
        S.ts("dve", P(NR), P(ABR), -1.0, None, ALU.add, None, R1, R1)
        S.tt("dve", P(T1), P(ARE), P(ARE), ALU.mult, R1, R1)
        S.tt("dve", P(T2), P(AIM), P(AIM), ALU.mult, R1, R1)
        S.tt("dve", P(DEN), P(T1), P(T2), ALU.add, R1, R1)
        S.op("dve", lambda e: e.reciprocal(out=P(RDEN), in_=P(DEN)), R1, R1)
        S.tt("dve", P(T1), P(NR), P(ARE), ALU.mult, R1, R1)
        S.tt("dve", P(T2), P(ABI), P(AIM), ALU.mult, R1, R1)
        S.tt("dve", P(T3), P(T1), P(T2), ALU.add, R1, R1)
        S.tt("dve", P(CORE), P(T3), P(RDEN), ALU.mult, R1, R1)
        S.tt("dve", P(T1), P(ABI), P(ARE), ALU.mult, R1, R1)
        S.tt("dve", P(T2), P(NR), P(AIM), ALU.mult, R1, R1)
        S.tt("dve", P(T3), P(T1), P(T2), ALU.subtract, R1, R1)
        S.tt("dve", P(COIM), P(T3), P(RDEN), ALU.mult, R1, R1)
        S.copy("dve", P(ER[0]), P(C1), R1, R1)
        S.copy("dve", P(EI[0]), P(S1), R1, R1)
        for k in range(1, 7):
            S.tt("dve", P(T1), P(ER[k - 1]), P(ER[k - 1]), ALU.mult, R1, R1)
            S.tt("dve", P(T2), P(EI[k - 1]), P(EI[k - 1]), ALU.mult, R1, R1)
            S.tt("dve", P(ER[k]), P(T1), P(T2), ALU.subtract, R1, R1)
            S.tt("dve", P(T1), P(ER[k - 1]), P(EI[k - 1]), ALU.mult, R1, R1)
            S.ts("dve", P(EI[k]), P(T1), 2.0, None, ALU.mult, None, R1, R1)
        S.memset("dve", TC[:, :, 0:1], 1.0, [TC])
        S.memset("dve", TSn[:, :, 0:1], 0.0, [TSn])
        S.copy("dve", TC[:, :, 1:2], P(ER[0]).unsqueeze(2), [prm], [TC])
        S.copy("dve", TSn[:, :, 1:2], P(EI[0]).unsqueeze(2), [prm], [TSn])
        for k in range(1, 6):
            n = 1 << k
            er = bc(P(ER[k]).unsqueeze(2), [128, 16, n])
            ei = bc(P(EI[k]).unsqueeze(2), [128, 16, n])
            self.cmul("dve", TC[:, :, n:2 * n], TSn[:, :, n:2 * n], TC[:, :, 0:n], TSn[:, :, 0:n], er, ei,
                      tA[:, :, 0:n], tB[:, :, 0:n], [prm, TC, TSn, tA, tB], [TC, TSn, tA, tB])
        cr = bc(P(CORE).unsqueeze(2), [128, 16, 128])
        ci = bc(P(COIM).unsqueeze(2), [128, 16, 128])
        self.cmul("dve", BBr[:], BBi[:], BPr[:], BPi[:], cr, ci, tC[:], tD[:], [prm, BPr, BPi, tC, tD, BBr, BBi], [BBr, BBi, tC, tD])
        n = 0
        for q in range(16):
            for (src, dst, part, sc) in ((BBr, BL, 0, None), (BBi, BL, 1, None), (CPr, CL, 0, None), (CPi, CL, 1, -1.0)):
                pbk = pb[n % 2]
                n += 1
                S.tr(pbk[:, 0:128], src[:, q, :], self.identf[:], [src, self.identf], [pbk])
                if sc is None:
                    S.copy("act", dst[:, q, part, :], pbk[:, 0:128], [pbk], [dst])
                else:
                    S.ts("dve", dst[:, q, part, :], pbk[:, 0:128], sc, None, ALU.mult, None, [pbk], [dst])
        for j in range(4):
            S.ts("dve", diagD[:, j, :], self.identf[:], dcol[:, j:j + 1], None, ALU.mult, None, [self.identf, dcol], [diagD])
        S.barrier()
        A.off = mark_setup

        bmr = A.alloc("bmr", [128, 16, 64])
        bmi = A.alloc("bmi", [128, 16, 64])
        xtr = [A.alloc("xtr%d" % i, [128, 16, 64]) for i in range(2)]
        xti = [A.alloc("xti%d" % i, [128, 16, 64]) for i in range(2)]
        Xb = [A.alloc("Xb%d" % i, [128, 2, 16, 64], BF16) for i in range(2)]
        m1 = A.alloc("m1", [128, 16, 64])
        m2 = A.alloc("m2", [128, 16, 64])
        RHOF = A.alloc("RHOF", [128, 16, 64])
        busR = A.alloc("busR", [128, 16, 64])
        busI = A.alloc("busI", [128, 16, 64])
        yraw = A.alloc("yraw", [128, 4, 512])
        g1 = A.alloc("g1", [128, 512])
        g2 = A.alloc("g2", [128, 512])
        ygb = A.alloc("ygb", [128, 4, 512], BF16)
        sg = [A.alloc("sg%d" % i, [128, 512]) for i in range(2)]
        bu_r = [Buf("bu_r%d" % h, self.psum[:, h * 512:(h + 1) * 512].rearrange("p (q j) -> p q j", q=8), lock=self.plocks[h]) for h in range(2)]
        bu_i = [Buf("bu_i%d" % h, self.psum[:, (2 + h) * 512:(3 + h) * 512].rearrange("p (q j) -> p q j", q=8), lock=self.plocks[2 + h]) for h in range(2)]
        ylock = self.plocks[4]
        yps = [Buf("yps%d" % h, self.psum[:, 4 * 512 + h * 256:4 * 512 + (h + 1) * 256].rearrange("p (a b) -> p a b", a=4), lock=ylock) for h in range(2)]
        ctmp = [Buf("s5ct%d" % i, P(44 + i)) for i in range(4)]
        IRb = [P(IR[0]), P(IR[1])]
        IIb = [P(II[0]), P(II[1])]
        S.memset("dve", IRb[0], 0.0, [prm])
        S.memset("dve", IIb[0], 0.0, [prm])
        S.copy("dve", RHOF[:], bc(P(MAG).unsqueeze(2), [128, 16, 64]), [prm], [RHOF])
        S.memset("dve", RHOF[:, :, 0:1], 0.0, [RHOF])
        S.tt("dve", P(TI[2]), P(ER[6]), P(MAG), ALU.mult, [prm], [prm])
        S.tt("dve", P(TI[3]), P(EI[6]), P(MAG), ALU.mult, [prm], [prm])
        nchunk = 0

        def bu(g):
            bb, cc = divmod(g, 8)
            uu_ = uT[bb % 2]
            c0_ = cc * 64
            for q in range(16):
                h_, ql = q // 8, q % 8
                S.mm(bu_r[h_][:, ql, :], BL[:, q, 0, :], uu_[:, q // 4, c0_:c0_ + 64], True, True, [BL, uu_], [bu_r[h_]])
                S.mm(bu_i[h_][:, ql, :], BL[:, q, 1, :], uu_[:, q // 4, c0_:c0_ + 64], True, True, [BL, uu_], [bu_i[h_]])
            for h_ in range(2):
                qs_ = slice(8 * h_, 8 * h_ + 8)
                S.copy("act", busR[:, qs_, :], bu_r[h_][:], [bu_r[h_]], [busR])
                S.copy("act", busI[:, qs_, :], bu_i[h_][:], [bu_i[h_]], [busI])

        bu(0)
        for b in range(8):
            own = b >= 4
            u_ = uT[b % 2]
            for ch in range(8):
                co = ch * 64
                cur = nchunk % 2
                nxt = (nchunk + 1) % 2
                S.tt("dve", bmr[:], busR[:], TC[:], ALU.mult, [busR, TC], [bmr])
                S.tt("dve", m1[:], busI[:], TSn[:], ALU.mult, [busI, TSn], [m1])
                S.tt("dve", bmi[:], busI[:], TC[:], ALU.mult, [busI, TC], [bmi])
                S.tt("dve", m2[:], busR[:], TSn[:], ALU.mult, [busR, TSn], [m2])
                S.tt("dve", bmr[:], bmr[:], m1[:], ALU.add, [bmr, m1], [bmr])
                S.tt("dve", bmi[:], bmi[:], m2[:], ALU.subtract, [bmi, m2], [bmi])
                S.tt("dve", bmr[:, :, 0], bmr[:, :, 0], IRb[cur], ALU.add, [bmr, prm], [bmr])
                S.tt("dve", bmi[:, :, 0], bmi[:, :, 0], IIb[cur], ALU.add, [bmi, prm], [bmi])
                if ch == 3 and b + 1 < 8:
                    front(b + 1)
                if nchunk + 1 < 64:
                    bu(nchunk + 1)
                xr, xi = xtr[cur], xti[cur]
                fl = lambda t_: t_.t.rearrange("p q j -> p (q j)")
                S.scan(fl(xr), fl(RHOF), fl(bmr), 0.0, [RHOF, bmr], [xr])
                S.scan(fl(xi), fl(RHOF), fl(bmi), 0.0, [RHOF, bmi], [xi])
                lr, li = xr[:, :, 63], xi[:, :, 63]
                S.tt("dve", ctmp[0].t, lr, P(TI[2]), ALU.mult, [prm, xr], [ctmp[0]])
                S.tt("dve", ctmp[1].t, li, P(TI[3]), ALU.mult, [prm, xi], [ctmp[1]])
                S.tt("dve", ctmp[2].t, lr, P(TI[3]), ALU.mult, [prm, xr], [ctmp[2]])
                S.tt("dve", ctmp[3].t, li, P(TI[2]), ALU.mult, [prm, xi], [ctmp[3]])
                S.tt("dve", IRb[nxt], ctmp[0].t, ctmp[1].t, ALU.subtract, [ctmp[0], ctmp[1]], [prm])
                S.tt("dve", IIb[nxt], ctmp[2].t, ctmp[3].t, ALU.add, [ctmp[2], ctmp[3]], [prm])
                if own:
                    X_ = Xb[cur]
                    S.tt("dve", m1[:], TC[:], xr[:], ALU.mult, [TC, xr], [m1])
                    S.tt("dve", m2[:], TSn[:], xi[:], ALU.mult, [TSn, xi], [m2])
                    S.tt("dve", bmr[:], TSn[:], xr[:], ALU.mult, [TSn, xr], [bmr])
                    S.tt("dve", bmi[:], TC[:], xi[:], ALU.mult, [TC, xi], [bmi])
                    S.tt("dve", X_[:, 0, :, :], m1[:], m2[:], ALU.subtract, [m1, m2], [X_])
                    S.tt("dve", X_[:, 1, :, :], bmr[:], bmi[:], ALU.add, [bmr, bmi], [X_])
                    yp = yps[cur]
                    for j in range(4):
                        first = True
                        for q in range(4 * j, 4 * j + 4):
                            S.mm(yp[:, j, :], CL[:, q, 0, :], X_[:, 0, q, :], first, False, [CL, X_], [yp])
                            first = False
                            S.mm(yp[:, j, :], CL[:, q, 1, :], X_[:, 1, q, :], False, False, [CL, X_], [yp])
                        S.mm(yp[:, j, :], diagD[:, j, :], u_[:, j, co:co + 64], False, True, [diagD, u_], [yp])
                    S.copy("act", yraw[:, :, co:co + 64], yp[:], [yp], [yraw])
                nchunk += 1
            if own:
                ob = self.OY[b - 4]
                for j in range(4):
                    S.act(g1[:], yraw[:, j, :], AF.Square, [yraw], [g1])
                    S.ts("dve", g1[:], g1[:], 0.044715, 1.0, ALU.mult, ALU.add, [g1], [g1])
                    S.tt("pool", g2[:], g1[:], yraw[:, j, :], ALU.mult, [g1, yraw], [g2])
                    S.act(g1[:], g2[:], AF.Sigmoid, [g2], [g1], scale=1.5957691216)
                    S.tt("pool", ygb[:, j, :], yraw[:, j, :], g1[:], ALU.mult, [yraw, g1], [ygb])
                for f in range(4):
                    pbk = pb[f % 2]
                    for c in range(4):
                        S.mm(pbk[:], wglu[:, c, f * 128:(f + 1) * 128], ygb[:, c, :], c == 0, c == 3, [wglu, ygb], [pbk])
                    s_ = sg[f % 2]
                    S.act(s_[:], pbk[:], AF.Sigmoid, [pbk, bglu], [s_], bias=bglu[:, f:f + 1])
                    S.tt("pool", ob[:, 4 + f, :], ygb[:, f, :], s_[:], ALU.mult, [ygb, s_], [ob])

    def sweep_dn(self):
        S, A, I = self.S, self.A, self.I
        NB = 256
        wdn = A.alloc("wdn", [128, 8, 2304], BF16)
        wblk = {}
        for nm_, c0_, c1_ in (("g", 2048, 2304), ("k", 512, 1024), ("v", 1024, 1536), ("q", 0, 512), ("z", 1536, 2048)):
            wblk[nm_] = Buf("wdn_" + nm_, wdn.t[:, :, c0_:c1_])
        S.memset("pool", wdn[:, :, 2176:2304], 0.0, [wblk["g"]])
        for k in range(8):
            S.dma("pool", wdn[:, k, 2048:2176], I["w_in"][0, k * 128:(k + 1) * 128, 2048:2176], writes=[wblk["g"]])
            S.dma("pool", wdn[:, k, 2176:2180], I["w_in"][0, k * 128:(k + 1) * 128, OFF_B:OFF_B + 4], writes=[wblk["g"]])
        for nm_, c0_, c1_ in (("k", 512, 1024), ("v", 1024, 1536), ("q", 0, 512), ("z", 1536, 2048)):
            for k in range(8):
                S.dma("pool", wdn[:, k, c0_:c1_], I["w_in"][0, k * 128:(k + 1) * 128, c0_:c1_], writes=[wblk[nm_]])

        def wbuf(c0):
            return wblk["g"] if c0 >= 2048 else wblk["q"] if c0 < 512 else wblk["k"] if c0 < 1024 else wblk["v"] if c0 < 1536 else wblk["z"]
        gain = A.alloc("gmix", [128, 1024])
        S.dma("sp", gain[:], I["norm_mix_g"][0:1, :].partition_broadcast(128), writes=[gain])
        cw = A.alloc("convw", [128, 12, 4])
        for j in range(4):
            S.dma("sp", cw[:, :, j], I["conv_w"][0, j].rearrange("(f p) -> p f", p=128), writes=[cw], slow=True)
        sm = A.alloc("dnsm", [128, 8])
        S.memset("dve", sm[:], 0.0, [sm])
        S.dma("sp", sm[0:4, 0:1], I["dn_dt_bias"][0].rearrange("(a o) -> a o", o=1), writes=[sm])
        S.dma("sp", sm[0:4, 1:2], I["dn_a_log"][0].rearrange("(a o) -> a o", o=1), writes=[sm])
        S.dma("sp", sm[:, 2:3], I["dn_norm_g"][0].rearrange("(a o) -> a o", o=1), writes=[sm])
        S.act(sm[0:4, 1:2], sm[0:4, 1:2], AF.Exp, [sm], [sm])
        S.ts("dve", sm[0:4, 1:2], sm[0:4, 1:2], -1.0, None, ALU.mult, None, [sm], [sm])
        S.memset("dve", sm[:, 3:4], math.log(128 ** -0.5), [sm])
        S.memset("dve", sm[:, 4:5], 1.0, [sm])

        xs = A.alloc("xs", [128, 1024])
        ss = A.alloc("ss", [128, 1])
        rs = A.alloc("rs", [128, 1])
        xn = A.alloc("xn", [128, 1024], BF16)
        xnT = A.alloc("xnT", [128, 8, NB], BF16)
        raw = A.alloc("raw", [128, 12, NB + 4], BF16)
        S.memset("dve", raw[:], 0.0, [raw])
        acc = [A.alloc("acc%d" % i, [128, NB]) for i in range(2)]
        sqb = [A.alloc("sqb%d" % i, [128, NB], BF16) for i in range(2)]
        rn = [A.alloc("rn%d" % i, [128, NB]) for i in range(2)]
        qT2 = [A.alloc("qT%d" % i, [128, 4, NB], BF16) for i in range(2)]
        kT2 = [A.alloc("kT%d" % i, [128, 4, NB], BF16) for i in range(2)]
        vT2 = [A.alloc("vT%d" % i, [128, 4, NB], BF16) for i in range(2)]
        zs2 = [A.alloc("zs%d" % i, [128, 4, NB], BF16) for i in range(2)]
        qgT2 = [[A.alloc("qgT%d_%d" % (i, h), [128, NB], BF16) for h in range(4)] for i in range(2)]
        rows = A.alloc("rows", [128, 9, NB])
        S.memset("pool", rows[:], 0.0, [rows])
        R_TMP, R_G, R_GC, R_BETA, R_EGC, R_AL, R_NGC, R_BEGE, R_EKD = range(9)
        Rw = lambda i: rows.t[0:4, i, :]
        cols2 = [A.alloc("cols%d" % i, [128, 2, 4, 4]) for i in range(2)]
        GB2 = [A.alloc("GB%d" % i, [128, 4, 2, NB]) for i in range(2)]
        GCb2 = [Buf("GCbv%d" % i, GB2[i].t[:, :, 0, :]) for i in range(2)]
        BEb2 = [Buf("BEbv%d" % i, GB2[i].t[:, :, 1, :]) for i in range(2)]
        EGb = A.alloc("EGb", [128, 4, NB])
        ALc2 = [A.alloc("ALc%d" % i, [128, 4, 4]) for i in range(2)]
        d1 = [A.alloc("sd1_%d" % i, [128, 128]) for i in range(4)]
        dec = [A.alloc("sdec_%d" % i, [128, 128]) for i in range(4)]
        decs = [A.alloc("sdecs_%d" % i, [128, 128]) for i in range(4)]
        tbf = d1
        Pm = [A.alloc("sP%d" % i, [128, 128]) for i in range(4)]
        Um = [A.alloc("sU%d" % i, [128, 128]) for i in range(4)]
        Wm = [A.alloc("sW%d" % i, [128, 128]) for i in range(4)]
        Wbf = [A.alloc("Wbf%d" % i, [128, 128], BF16) for i in range(8)]
        RHSv = [A.alloc("RHSv%d" % i, [128, 128], BF16) for i in range(8)]
        RHSk = [A.alloc("RHSk%d" % i, [128, 128], BF16) for i in range(8)]
        kdec = [A.alloc("kdec%d" % i, [128, 128], BF16) for i in range(8)]
        uu = [A.alloc("uu%d" % i, [128, 128]) for i in range(8)]
        wT = [A.alloc("wT%d" % i, [128, 128], BF16) for i in range(8)]
        attnT = [A.alloc("attnT%d" % i, [128, 128], BF16) for i in range(8)]
        S32 = [A.alloc("S32_%d" % i, [128, 128]) for i in range(4)]
        Sbf = [[A.alloc("Sbf%d_%d" % (i, j), [128, 128], BF16) for j in range(3)] for i in range(4)]
        vnA = [A.alloc("vnA%d" % i, [128, 128], BF16) for i in range(4)]
        vnB = [A.alloc("vnB%d" % i, [128, 128], BF16) for i in range(4)]
        oraw = [A.alloc("oraw%d" % i, [128, NB]) for i in range(4)]
        for h in range(4):
            S.memset("dve", S32[h][:], 0.0, [S32[h]])
            S.memset("dve", Sbf[h][0][:], 0.0, [Sbf[h][0]])
            S.memset("pool", vnA[h][:], 0.0, [vnA[h]])
            S.memset("pool", vnB[h][:], 0.0, [vnB[h]])
        tb = self.bank(0, "dn_tb")
        pacc = [Buf("dn_pacc%d" % i, self.psum[:, (1 + i) * 512:(1 + i) * 512 + 256], lock=self.plocks[1 + i]) for i in range(2)]
        pn_ = Buf("dn_pnrm", self.psum[:, 3 * 512:3 * 512 + 256], lock=self.plocks[3])
        pnrm = [pn_, pn_]
        hlock = [self.plocks[4 + h] for h in range(4)]

        def slots(j, nm):
            return [Buf("%s%d" % (nm, h), self.psum[:, (4 + h) * 512 + j * 128:(4 + h) * 512 + (j + 1) * 128], lock=hlock[h]) for h in range(4)]
        pk, pu, pp, pw = slots(0, "pk"), slots(1, "pu"), slots(2, "pp"), slots(3, "pw")
        pq = pu
        identf, identb, sel = self.identf, self.identb, self.sel
        nacc = 0
        npair = 0
        FS = [dict(qT=qT2[i], kT=kT2[i], vT=vT2[i], zs=zs2[i], qgT=qgT2[i], GCb=GB2[i], BEb=GB2[i], cols=cols2[i], ALc=ALc2[i])
              for i in range(2)]
        GCv = {id(GB2[i]): GCb2[i].t for i in range(2)}
        BEv = {id(GB2[i]): BEb2[i].t for i in range(2)}

        def proj(c0):
            nonlocal nacc
            p_ = pacc[nacc % 2]
            nacc += 1
            wb_ = wbuf(c0)
            for k in range(8):
                S.mm(p_[:], wdn[:, k, c0:c0 + 128], xnT[:, k, :], k == 0, k == 7, [wb_, xnT], [p_])
            return p_

        def front(b, F):
            own = b >= 8
            qT, kT, vT, zs, qgT, GCb, BEb, cols, ALc = (F[n_] for n_ in ("qT", "kT", "vT", "zs", "qgT", "GCb", "BEb", "cols", "ALc"))
            for i in range(2):
                S.dma("sp", xs[:], I["xe"][b * NB + i * 128:b * NB + (i + 1) * 128, :], writes=[xs])
                self.norm_to_T(xs[:], xs, gain, xnT, i * 128, tb, (xn, ss, rs, xn))
                yield "any"
            p_ = proj(OFF_A)
            S.act(Rw(R_TMP), p_[0:4, :], AF.Exp, [p_, sm], [rows], bias=sm[0:4, 0:1])
            S.act(Rw(R_TMP), Rw(R_TMP), AF.Ln, [rows, sm], [rows], bias=sm[0:4, 4:5])
            S.ts("dve", Rw(R_G), Rw(R_TMP), sm[0:4, 1:2], None, ALU.mult, None, [rows, sm], [rows])
            yield "any"
            p_ = proj(2176)
            S.act(Rw(R_BETA), p_[0:4, :], AF.Exp, [p_], [rows], scale=-1.0)
            S.act(Rw(R_BETA), Rw(R_BETA), AF.Ln, [rows, sm], [rows], bias=sm[0:4, 4:5])
            S.act(Rw(R_BETA), Rw(R_BETA), AF.Exp, [rows], [rows], scale=-1.0)
            yield "any"
            S.scan(Rw(R_GC), self.resetm[0:4, :], Rw(R_G), 0.0, [self.resetm, rows], [rows])
            S.ts("dve", Rw(R_NGC), Rw(R_GC), -1.0, None, ALU.mult, None, [rows], [rows])
            S.act(Rw(R_EGC), Rw(R_GC), AF.Exp, [rows], [rows])
            S.tt("dve", Rw(R_BEGE), Rw(R_BETA), Rw(R_EGC), ALU.mult, [rows], [rows])
            gc3 = Rw(R_GC).rearrange("p (c j) -> p c j", j=64)
            gl = bc(gc3[:, :, 63:64], [4, 4, 64])
            S.tt("dve", Rw(R_TMP).rearrange("p (c j) -> p c j", j=64), gl, gc3, ALU.subtract, [rows], [rows])
            S.act(Rw(R_EKD), Rw(R_TMP), AF.Exp, [rows], [rows])
            S.act(Rw(R_AL).rearrange("p (c j) -> p c j", j=64), gl, AF.Exp, [rows], [rows])
            yield "pe"
            tb4 = tb.t.rearrange("p (q c) -> p q c", q=4)
            for i in range(2):
                for qi, ri in enumerate((R_NGC, R_BETA, R_BEGE, R_EKD)):
                    S.tr(tb4[:, qi, :], rows[:, ri, i * 128:(i + 1) * 128], identf[:], [rows, identf], [tb])
                S.copy("dve", cols[:, i, :, :], tb4[:, :, 0:4], [tb], [cols])
                yield "pe"
            pnf = self.psum[:, 3 * 512:4 * 512]
            for h in range(4):
                S.mm(pnf, sel[:, h, :], rows[:, R_GC:R_GC + 2, :], True, True, [sel, rows], [pn_])
                S.copy("act", GCb[:, h, :, :], pnf.rearrange("p (a b) -> p a b", a=2), [pn_], [GCb])
                S.mm(tb[:], sel[:, h, :], rows[:, R_EGC:R_EGC + 2, :], True, True, [sel, rows], [tb])
                if own:
                    S.copy("act", EGb[:, h, :], tb[:, 0:NB], [tb], [EGb])
                S.copy("dve", ALc[:, h, :], tb[:, NB:2 * NB:64], [tb], [ALc])
                yield "pe"
            groups = [("k", h) for h in range(4)] + [("v", h) for h in range(4)]
            if own:
                groups += [("q", h) for h in range(4)]
            if b == 7:
                groups += [("qh", h) for h in range(4)]
            if own:
                groups += [("z", h) for h in range(4)]
            def do_proj(kind, h):
                base = {"q": OFF_Q, "k": OFF_K, "v": OFF_V, "z": OFF_Z, "qh": OFF_Q}[kind]
                p_ = proj(base + h * 128)
                if kind == "qh":
                    S.copy("act", raw[:, h, 3:3 + NB], p_[:], [p_], [raw])
                    S.copy("pool", raw[:, h, 0:3], raw[:, h, NB:NB + 3], [raw], [raw])
                    return None
                if kind == "z":
                    S.act(zs[:, h, :], p_[:], AF.Silu, [p_], [zs])
                    return None
                fb = {"q": 0, "k": 4, "v": 8}[kind] + h
                S.copy("act", raw[:, fb, 3:3 + NB], p_[:], [p_], [raw])
                return (kind, h, fb)

            def do_conv(kind, h, fb):
                a_ = acc[fb % 2]
                S.ts("dve", a_[:], raw[:, fb, 0:NB], cw[:, fb, 0:1], None, ALU.mult, None, [raw, cw], [a_])
                for j in range(1, 4):
                    S.stt("dve", a_[:], raw[:, fb, j:j + NB], cw[:, fb, j:j + 1], a_[:], ALU.mult, ALU.add, [raw, cw, a_], [a_])
                S.copy("pool", raw[:, fb, 0:3], raw[:, fb, NB:NB + 3], [raw], [raw])
                dst = {"q": qT, "k": kT, "v": vT}[kind]
                S.act(dst[:, h, :], a_[:], AF.Silu, [a_], [dst])

            convs = []
            pend = []
            for gi, (kind, h) in enumerate(groups):
                r_ = do_proj(kind, h)
                if r_ is not None:
                    convs.append(r_)
                    pend.append(r_)
                if len(pend) >= 2:
                    yield "ve"
                    do_conv(*pend.pop(0))
                yield "pe" if gi + 1 < len(groups) else "ve"
            while pend:
                do_conv(*pend.pop(0))
                yield "ve"
            for (kind, h, fb) in convs:
                if kind == "v":
                    continue
                dst = {"q": qT, "k": kT}[kind]
                sq_ = sqb[fb % 2]
                S.act(sq_[:], dst[:, h, :], AF.Square, [dst], [sq_])
                S.mm(pn_[:], self.onesb[:], sq_[:], True, True, [self.onesb, sq_], [pn_])
                r_ = rn[fb % 2]
                S.act(r_[:], pn_[:], AF.Ln, [pn_], [r_], bias=self.epsc[:, 0:1])
                if kind == "q":
                    S.act(r_[:], r_[:], AF.Exp, [r_, sm], [r_], scale=-0.5, bias=sm[:, 3:4])
                else:
                    S.act(r_[:], r_[:], AF.Exp, [r_], [r_], scale=-0.5)
                S.tt("dve", dst[:, h, :], dst[:, h, :], r_[:], ALU.mult, [dst, r_], [dst])
                yield "ve"
            if own:
                for h in range(4):
                    S.tt("pool", qgT[h][:], qT[:, h, :], EGb[:, h, :], ALU.mult, [qT, EGb], [qgT[h]])
                yield "ve"

        gen = None
        nkind = "any"

        def pump(kind=None):
            nonlocal gen, nkind
            if gen is None:
                return
            if kind is not None and nkind != "any" and nkind != kind:
                return
            try:
                nkind = next(gen)
            except StopIteration:
                gen = None

        def back(b, F):
            nonlocal npair
            own = b >= 8
            qT, kT, vT, zs, qgT, GCb, BEb, cols, ALc = (F[n_] for n_ in ("qT", "kT", "vT", "zs", "qgT", "GCb", "BEb", "cols", "ALc"))
            H4 = range(4)
            for pt in range(2):
                sl = slice(pt * 128, (pt + 1) * 128)
                ix = lambda h: pt * 4 + h
                for h in H4:
                    S.mm(pk[h][:], kT[:, h, sl], kT[:, h, sl], True, True, [kT], [pk[h]])
                for h in H4:
                    S.stt("dve", d1[h][:], GCv[id(GCb)][:, h, sl], cols[:, pt, 0, h:h + 1], self.negmask[:], ALU.add, ALU.add,
                          [GCb, cols, self.negmask], [d1[h]])
                    S.act(dec[h][:], d1[h][:], AF.Exp, [d1[h]], [dec[h]])
                    S.tt("pool", decs[h][:], dec[h][:], self.strictm[:], ALU.mult, [dec[h], self.strictm], [decs[h]])
                pump("pe")
                for h in H4:
                    S.tt("dve", tbf[h][:], pk[h][:], BEv[id(BEb)][:, h, sl], ALU.mult, [pk[h], BEb], [tbf[h]])
                    S.stt("dve", Pm[h][:], tbf[h][:], -1.0, decs[h][:], ALU.mult, ALU.mult, [tbf[h], decs[h]], [Pm[h]])
                if own:
                    for h in H4:
                        S.mm(pq[h][:], kT[:, h, sl], qT[:, h, sl], True, True, [kT, qT], [pq[h]])
                    for h in H4:
                        S.tt("dve", attnT[ix(h)][:], pq[h][:], dec[h][:], ALU.mult, [pq[h], dec[h]], [attnT[ix(h)]])
                pump("pe")
                for h in H4:
                    S.tr(pu[h][:], Pm[h][:], identf[:], [Pm[h], identf], [pu[h]])
                for h in H4:
                    S.copy("act", Um[h][:], pu[h][:], [pu[h]], [Um[h]])
                    S.tt("pool", Wm[h][:], Pm[h][:], identf[:], ALU.add, [Pm[h], identf], [Wm[h]])
                pump("pe")
                for k in range(6):
                    for h in H4:
                        if k < 5:
                            S.mm(pu[h][:], Pm[h][:], Um[h][:], True, True, [Pm[h], Um[h]], [pu[h]])
                        if k < 4:
                            S.mm(pp[h][:], Um[h][:], Pm[h][:], True, True, [Pm[h], Um[h]], [pp[h]])
                    for h in H4:
                        if k >= 1:
                            S.mm(pw[h][:], Um[h][:], Wm[h][:], True, True, [Um[h], Wm[h]], [pw[h]])
                    pump("ve")
                    for h in H4:
                        if k < 5:
                            S.copy("act", Um[h][:], pu[h][:], [pu[h]], [Um[h]])
                        if k < 4:
                            S.copy("dve", Pm[h][:], pp[h][:], [pp[h]], [Pm[h]])
                    for h in H4:
                        if k >= 1:
                            dstW = Wbf[ix(h)] if k == 5 else Wm[h]
                            S.tt("dve", dstW[:], pw[h][:], Wm[h][:], ALU.add, [pw[h], Wm[h]], [dstW])
                    pump("pe")
                for h in H4:
                    pkb = pk[h].t.bitcast(BF16)[:, 0:128]
                    pqb = pq[h].t.bitcast(BF16)[:, 0:128]
                    S.tr(pkb, kT[:, h, sl], identb[:], [kT, identb], [pk[h]])
                    S.tr(pqb, vT[:, h, sl], identb[:], [vT, identb], [pq[h]])
                pump("ve")
                for h in H4:
                    pkb = pk[h].t.bitcast(BF16)[:, 0:128]
                    pqb = pq[h].t.bitcast(BF16)[:, 0:128]
                    S.ts("dve", RHSk[ix(h)][:], pkb, cols[:, pt, 2, h:h + 1], None, ALU.mult, None, [pk[h], cols], [RHSk[ix(h)]])
                    S.act(kdec[ix(h)][:], pkb, AF.Copy, [pk[h], cols], [kdec[ix(h)]], scale=cols[:, pt, 3, h:h + 1])
                    S.ts("dve", RHSv[ix(h)][:], pqb, cols[:, pt, 1, h:h + 1], None, ALU.mult, None, [pq[h], cols], [RHSv[ix(h)]])
                pump("pe")
                for h in H4:
                    S.mm(pu[h][:], Wbf[ix(h)][:], RHSv[ix(h)][:], True, True, [Wbf[ix(h)], RHSv[ix(h)]], [pu[h]])
                    S.mm(pp[h][:], RHSk[ix(h)][:], Wbf[ix(h)][:], True, True, [Wbf[ix(h)], RHSk[ix(h)]], [pp[h]])
                pump("ve")
                for h in H4:
                    S.copy("act", uu[ix(h)][:], pu[h][:], [pu[h]], [uu[ix(h)]])
                    S.copy("dve", wT[ix(h)][:], pp[h][:], [pp[h]], [wT[ix(h)]])
                pump("pe")
                r0 = (2 * npair) % 3
                for ci in range(2):
                    rws = slice(ci * 64, (ci + 1) * 64)
                    rc, rnx = (r0 + ci) % 3, (r0 + ci + 1) % 3
                    vn = vnA if ci == 0 else vnB
                    for h in H4:
                        S.mm(pw[h][:], wT[ix(h)][:], Sbf[h][rc][:], True, True, [wT[ix(h)], Sbf[h][rc]], [pw[h]])
                    pump("ve")
                    for h in H4:
                        S.tt("dve", vn[h][rws, :], uu[ix(h)][rws, :], pw[h][rws, :], ALU.subtract, [uu[ix(h)], pw[h]], [vn[h]])
                    pump("pe")
                    for h in H4:
                        S.mm(pq[h][:], kdec[ix(h)][:], vn[h][:], True, True, [kdec[ix(h)], vn[h]], [pq[h]])
                    pump("ve")
                    for h in H4:
                        al_ = ALc[:, h, pt * 2 + ci:pt * 2 + ci + 1]
                        S.stt("dve", Sbf[h][rnx][:], S32[h][:], al_, pq[h][:], ALU.mult, ALU.add, [S32[h], ALc, pq[h]], [Sbf[h][rnx]])
                        S.stt("dve", S32[h][:], S32[h][:], al_, pq[h][:], ALU.mult, ALU.add, [S32[h], ALc, pq[h]], [S32[h]])
                    pump("pe")
                if own:
                    t0 = pt * 128
                    for h in H4:
                        S.mm(pk[h][:, 0:64], Sbf[h][r0][:], qgT[h][:, t0:t0 + 64], True, False, [Sbf[h][r0], qgT[h]], [pk[h]])
                        S.mm(pk[h][:, 64:128], Sbf[h][(r0 + 1) % 3][:], qgT[h][:, t0 + 64:t0 + 128], False, False,
                             [Sbf[h][(r0 + 1) % 3], qgT[h]], [pk[h]])
                        S.mm(pk[h][:], vnA[h][:], attnT[ix(h)][:], False, False, [vnA[h], attnT[ix(h)]], [pk[h]])
                        S.mm(pk[h][:], vnB[h][:], attnT[ix(h)][:], False, True, [vnB[h], attnT[ix(h)]], [pk[h]])
                    for h in H4:
                        S.copy("act", oraw[h][:, sl], pk[h][:], [pk[h]], [oraw[h]])
                    pump("pe")
                npair += 1
            if own:
                blk = (b - 8) // 2
                c0 = ((b - 8) % 2) * NB
                ob = self.OY[blk]
                for h in range(4):
                    sq_ = sqb[h % 2]
                    S.act(sq_[:], oraw[h][:], AF.Square, [oraw[h]], [sq_])
                    S.mm(pn_[:], self.onesb[:], sq_[:], True, True, [self.onesb, sq_], [pn_])
                    r_ = rn[h % 2]
                    S.act(r_[:], pn_[:], AF.Ln, [pn_], [r_], scale=1.0 / 128, bias=self.epsc[:, 0:1])
                    S.act(r_[:], r_[:], AF.Exp, [r_], [r_], scale=-0.5)
                    q_ = acc[h % 2]
                    S.tt("dve", q_[:], oraw[h][:], r_[:], ALU.mult, [oraw[h], r_], [q_])
                    S.stt("dve", ob[:, h, c0:c0 + NB], q_[:], sm[:, 2:3], zs[:, h, :], ALU.mult, ALU.mult, [q_, sm, zs], [ob])

        import os
        nblk = int(os.environ.get("DN_NBLK", "16"))
        gen = front(0, FS[0])
        while gen is not None:
            pump()
        for b in range(nblk):
            gen = front(b + 1, FS[(b + 1) % 2]) if b + 1 < nblk else None
            nkind = "any"
            back(b, FS[b % 2])
            while gen is not None:
                pump()

    def phase_x0(self):
        S, A, I = self.S, self.A, self.I
        hall = A.alloc("h", [128, 16, 1024])
        self.H = [Buf("h%d" % i, hall.t[:, i, :]) for i in range(16)]
        self.mark_h = A.off
        wout = A.alloc("wout", [128, 8, 1024], BF16)
        for k in range(8):
            S.dma("pool", wout[:, k, :], I["w_out"][0, k * 128:(k + 1) * 128, :], writes=[wout])
        self.wq = A.alloc("wq", [128, 8, 1024], BF16)
        self.wo = A.alloc("wo", [128, 8, 1024], BF16)
        self.off_after_w = A.off
        for k in range(8):
            S.dma("pool", self.wq[:, k, :], I["w_xq"][0, k * 128:(k + 1) * 128, :], writes=[self.wq])
            S.dma("pool", self.wo[:, k, :], I["w_xo"][0, k * 128:(k + 1) * 128, :], writes=[self.wo])
        pb = [self.bank(i, "x0b%d" % i) for i in range(4)]
        for i in range(16):
            hb = self.H[i]
            S.dma("sp", hb[:], I["xe"][T_OWN + i * 128:T_OWN + (i + 1) * 128, :], writes=[hb])
            ob = self.OY[i // 4]
            c0 = (i % 4) * 128
            for hf in range(2):
                p_ = pb[(i * 2 + hf) % 4]
                for k in range(8):
                    S.mm(p_[:], ob[:, k, c0:c0 + 128], wout[:, k, hf * 512:(hf + 1) * 512], k == 0, k == 7, [ob, wout], [p_])
                S.tt("dve", hb[:, hf * 512:(hf + 1) * 512], hb[:, hf * 512:(hf + 1) * 512], p_[:], ALU.add, [hb, p_], [hb])
        A.off = self.mark_h

    def phase_x1(self):
        S, A, I = self.S, self.A, self.I
        A0 = Arena(self.A.t, 8192)
        wK = A0.alloc("wK", [128, 8, 1024], BF16)
        wV = A0.alloc("wV", [128, 8, 1024], BF16)
        gain = A.alloc("gx", [128, 1024])
        xn = A.alloc("xn", [128, 1024], BF16)
        junk = xn
        xs = A.alloc("xs", [128, 1024])
        mnT = A.alloc("mnT", [128, 8, NMEM], BF16)
        ss = A.alloc("ss", [128, 1])
        rs = A.alloc("rs", [128, 1])
        assert A.off <= self.mark_h + 4096
        A.off = self.off_after_w
        KT = A.alloc("KT", [128, 8, NMEM], BF16)
        Vm = A.alloc("Vm", [128, 2, 1024], BF16)
        xnT = A.alloc("xnT", [128, 8, 512], BF16)
        qxT = A.alloc("qxT", [128, 8, 512], BF16)
        tb = self.bank(0, "x1tb")
        pb = [self.bank(i, "x1b%d" % i) for i in range(1, 8)]
        npb = 0

        def nb():
            nonlocal npb
            npb += 1
            return pb[npb % 7]
        S.dma("sp", gain[:], I["norm_mem_g"][0:1, :].partition_broadcast(128), writes=[gain])
        for k in range(8):
            S.dma("pool", wK[:, k, :], I["w_xk"][0, k * 128:(k + 1) * 128, :], writes=[wK])
        for k in range(8):
            S.dma("pool", wV[:, k, :], I["w_xv"][0, k * 128:(k + 1) * 128, :], writes=[wV])
        for i in range(2):
            S.dma("sp", xs[:], I["mem"][i * 128:(i + 1) * 128, :], writes=[xs])
            self.norm_to_T(xs[:], xs, gain, mnT, i * 128, tb, (junk, ss, rs, xn))
        for cb in range(8):
            p_ = nb()
            for k in range(8):
                S.mm(p_[:, 0:NMEM], wK[:, k, cb * 128:(cb + 1) * 128], mnT[:, k, :], k == 0, k == 7, [wK, mnT], [p_])
            S.copy("act", KT[:, cb, :], p_[:, 0:NMEM], [p_], [KT])
        for mt in range(2):
            for hf in range(2):
                p_ = nb()
                for k in range(8):
                    S.mm(p_[:], mnT[:, k, mt * 128:(mt + 1) * 128], wV[:, k, hf * 512:(hf + 1) * 512], k == 0, k == 7, [wV, mnT], [p_])
                S.copy("act", Vm[:, mt, hf * 512:(hf + 1) * 512], p_[:], [p_], [Vm])
        S.barrier()
        A0 = Arena(self.A.t, 8192)
        ET = [A0.alloc("ET%d" % i, [128, 2, 512], BF16) for i in range(2)]
        rinv = [A0.alloc("rinv%d" % i, [128, 512]) for i in range(2)]
        oxT = A0.alloc("oxT", [128, 8, 512], BF16)
        wA, wo = self.wq, self.wo
        S.dma("sp", gain[:], I["norm_x_g"][0:1, :].partition_broadcast(128), writes=[gain])
        xnT2 = [xnT, A.alloc("xnTb", [128, 8, 512], BF16)]
        qxT2 = [qxT, A.alloc("qxTb", [128, 8, 512], BF16)]

        def xfront(blk):
            xnT_, qxT_ = xnT2[blk % 2], qxT2[blk % 2]
            for i in range(4):
                hb = self.H[blk * 4 + i]
                self.norm_to_T(hb[:], hb, gain, xnT_, i * 128, tb, (junk, ss, rs, xn))
                yield
            for cb in range(8):
                p_ = nb()
                for k in range(8):
                    S.mm(p_[:], wA[:, k, cb * 128:(cb + 1) * 128], xnT_[:, k, :], k == 0, k == 7, [wA, xnT_], [p_])
                S.copy("act", qxT_[:, cb, :], p_[:], [p_], [qxT_])
                yield

        gen = None

        def pump():
            nonlocal gen
            if gen is None:
                return
            try:
                next(gen)
            except StopIteration:
                gen = None

        gen = xfront(0)
        while gen is not None:
            pump()
        for blk in range(4):
            qxT_ = qxT2[blk % 2]
            gen = xfront(blk + 1) if blk + 1 < 4 else None
            for hd in range(4):
                E_ = ET[hd % 2]
                for mt in range(2):
                    p_ = nb()
                    for hf in range(2):
                        S.mm(p_[:], KT[:, hd * 2 + hf, mt * 128:(mt + 1) * 128], qxT_[:, hd * 2 + hf, :], hf == 0, hf == 1, [KT, qxT_], [p_])
                    S.act(E_[:, mt, :], p_[:], AF.Exp, [p_], [E_], scale=1.0 / 16.0)
                pump()
                p_ = nb()
                for mt in range(2):
                    S.mm(p_[:], self.onesb[:], E_[:, mt, :], mt == 0, mt == 1, [self.onesb, E_], [p_])
                r_ = rinv[hd % 2]
                S.act(r_[:], p_[:], AF.Ln, [p_], [r_])
                S.act(r_[:], r_[:], AF.Exp, [r_], [r_], scale=-1.0)
                pump()
                for hf in range(2):
                    p_ = nb()
                    for mt in range(2):
                        S.mm(p_[:], Vm[:, mt, hd * 256 + hf * 128:hd * 256 + (hf + 1) * 128], E_[:, mt, :], mt == 0, mt == 1, [Vm, E_], [p_])
                    S.tt("dve", oxT[:, hd * 2 + hf, :], p_[:], r_[:], ALU.mult, [p_, r_], [oxT])
            for i in range(4):
                hb = self.H[blk * 4 + i]
                for hf in range(2):
                    p_ = nb()
                    for c in range(8):
                        S.mm(p_[:], oxT[:, c, i * 128:(i + 1) * 128], wo[:, c, hf * 512:(hf + 1) * 512], c == 0, c == 7, [oxT, wo], [p_])
                    S.tt("dve", hb[:, hf * 512:(hf + 1) * 512], hb[:, hf * 512:(hf + 1) * 512], p_[:], ALU.add, [hb, p_], [hb])
                pump()
            while gen is not None:
                pump()
        A.off = self.mark_h

    def phase_f(self):
        S, A, I = self.S, self.A, self.I
        A0 = Arena(self.A.t, 8192)
        gain = A0.alloc("gf", [128, 1024])
        junk = A0.alloc("junk", [128, 1024], BF16)
        xn = A0.alloc("xn", [128, 1024], BF16)
        sgl = [A0.alloc("sgl%d" % i, [128, 512]) for i in range(2)]
        hid = [A0.alloc("hid%d" % i, [128, 4, 512], BF16) for i in range(2)]
        ost = [A0.alloc("ost%d" % i, [128, 1024]) for i in range(2)]
        ss = A.alloc("ss", [128, 1])
        rs = A.alloc("rs", [128, 1])
        xnT = A.alloc("xnTall", [128, 8, T_OWN], BF16)
        wg = [A.alloc("wg%d" % i, [128, 8, 512], BF16) for i in range(2)]
        wuU = [A.alloc("wup%d" % i, [128, 8, 512], BF16) for i in range(2)]
        wd = [A.alloc("wd%d" % i, [128, 4, 1024], BF16) for i in range(2)]
        tb = self.bank(0, "ftb")
        pb = [self.bank(i, "fb%d" % i) for i in range(1, 8)]
        npb = 0

        def nb():
            nonlocal npb
            npb += 1
            return pb[npb % 7]
        S.dma("sp", gain[:], I["norm_ffn_g"][0:1, :].partition_broadcast(128), writes=[gain])
        gfin = A.alloc("gfin", [128, 1024])
        S.dma("sp", gfin[:], I["norm_final_g"].rearrange("(o d) -> o d", o=1).partition_broadcast(128), writes=[gfin])
        slices = [(c0, min(4, 22 - c0)) for c0 in range(0, 22, 4)]

        def load_slice(si):
            c0, nc_ = slices[si]
            w = nc_ * 128
            for k in range(8):
                S.dma("pool", wg[si % 2][:, k, 0:w], I["w_gate"][0, k * 128:(k + 1) * 128, c0 * 128:c0 * 128 + w], writes=[wg[si % 2]])
                S.dma("pool", wuU[si % 2][:, k, 0:w], I["w_up"][0, k * 128:(k + 1) * 128, c0 * 128:c0 * 128 + w], writes=[wuU[si % 2]])
            for c in range(nc_):
                S.dma("pool", wd[si % 2][:, c, :], I["w_down"][0, (c0 + c) * 128:(c0 + c + 1) * 128, :], writes=[wd[si % 2]])
        load_slice(0)
        for i in range(16):
            self.norm_to_T(self.H[i][:], self.H[i], gain, xnT, i * 128, tb, (junk, ss, rs, xn))
        for si in range(len(slices)):
            if si + 1 < len(slices):
                load_slice(si + 1)
            c0, nc_ = slices[si]
            g_, u_, d_ = wg[si % 2], wuU[si % 2], wd[si % 2]
            for blk in range(4):
                tsl = slice(blk * 512, (blk + 1) * 512)
                hd_ = hid[blk % 2]
                for c in range(nc_):
                    pg = nb()
                    for k in range(8):
                        S.mm(pg[:], g_[:, k, c * 128:(c + 1) * 128], xnT[:, k, tsl], k == 0, k == 7, [g_, xnT], [pg])
                    pu_ = nb()
                    for k in range(8):
                        S.mm(pu_[:], u_[:, k, c * 128:(c + 1) * 128], xnT[:, k, tsl], k == 0, k == 7, [u_, xnT], [pu_])
                    s_ = sgl[c % 2]
                    S.act(s_[:], pg[:], AF.Silu, [pg], [s_])
                    S.tt("dve", hd_[:, c, :], s_[:], pu_[:], ALU.mult, [s_, pu_], [hd_])
                for i in range(4):
                    hb = self.H[blk * 4 + i]
                    for hf in range(2):
                        p_ = nb()
                        for c in range(nc_):
                            S.mm(p_[:], hd_[:, c, i * 128:(i + 1) * 128], d_[:, c, hf * 512:(hf + 1) * 512], c == 0, c == nc_ - 1, [hd_, d_], [p_])
                        S.tt("dve", hb[:, hf * 512:(hf + 1) * 512], hb[:, hf * 512:(hf + 1) * 512], p_[:], ALU.add, [hb, p_], [hb])
                if si == len(slices) - 1:
                    for i in range(4):
                        t_ = blk * 4 + i
                        hb = self.H[t_]
                        o_ = ost[t_ % 2]
                        self.rms_stats(hb[:], hb, junk, ss, rs)
                        S.stt("dve", o_[:], hb[:], rs[:, 0:1], gfin[:], ALU.mult, ALU.mult, [hb, rs, gfin], [o_])
                        S.dma("sp", self.y[t_ * 128:(t_ + 1) * 128, :], o_[:], reads=[o_])
        self.final.extend(ost)


_CACHE = {}


def _core_inputs(inputs, c):
    b, half = c // 2, c % 2
    x = inputs["x"]
    xe = np.zeros((T_EXT, D), np.float32)
    if half == 1:
        xe[:T_OWN] = x[b, :T_OWN]
        xe[T_OWN:] = x[b, T_OWN:]
    else:
        xe[T_OWN:] = x[b, :T_OWN]
    m = {"xe": xe, "mem": np.ascontiguousarray(inputs["mem"][b], dtype=np.float32)}
    for k, v in inputs.items():
        if k in ("x", "mem"):
            continue
        m[k] = np.ascontiguousarray(v, dtype=np.float32)
    return m


def kernel(**inputs):
    inputs = {k: np.asarray(v) for k, v in inputs.items()}
    if "prog" not in _CACHE:
        _CACHE["prog"] = Prog()
    prog = _CACHE["prog"]
    in_maps = [_core_inputs(inputs, c) for c in range(8)]
    res = run_bass_kernel_spmd(prog.nc, in_maps, core_ids=list(range(8)))
    out = np.zeros((4, 4096, D), np.float32)
    for c in range(8):
        b, half = c // 2, c % 2
        out[b, half * T_OWN:(half + 1) * T_OWN] = res.results[c]["y"]
    return out
```
